# Optimizing a Trainium2 kernel written in Bass

```python
import jax, jax.numpy as jnp
from jax import lax
import numpy as np

D_MODEL = 1024
BATCH = 4
SEQ = 8192
DEPTH = 1

N_META = 16
GLA_HEADS = 4
GLA_DK = D_MODEL // 2
GLA_DV = D_MODEL
GLA_DKH = GLA_DK // GLA_HEADS
GLA_DVH = GLA_DV // GLA_HEADS
GLA_RANK = 16
GATE_TAU = 16.0
CHUNK = 64
META_PAD = CHUNK - N_META
CONF_CH = D_MODEL
CONF_K = 31
D_FF = 2816
FFN_K = 3
IN_WIDTHS = (GLA_DK, GLA_DK, GLA_DV, GLA_DV, GLA_RANK, 2 * CONF_CH, D_MODEL, D_MODEL)
N_IN = sum(IN_WIDTHS)
RMS_EPS = 1e-6
LN_EPS = 1e-5

kernel_name = "gla_conformer_gated_hybrid_block"


def split_points(widths):
    pts, acc = [], 0
    for w in widths[:-1]:
        acc += w
        pts.append(acc)
    return pts


def rms_norm(x, g):
    xf = x.astype(jnp.float32)
    y = xf * lax.rsqrt(jnp.mean(xf * xf, axis=-1, keepdims=True) + RMS_EPS)
    return (y * g.astype(jnp.float32)).astype(x.dtype)


def layer_norm(x, g, b):
    xf = x.astype(jnp.float32)
    mu = jnp.mean(xf, axis=-1, keepdims=True)
    var = jnp.mean(jnp.square(xf - mu), axis=-1, keepdims=True)
    y = (xf - mu) * lax.rsqrt(var + LN_EPS)
    return (y * g.astype(jnp.float32) + b.astype(jnp.float32)).astype(x.dtype)


def causal_dwconv(x, w, b):
    K, C = w.shape
    out = lax.conv_general_dilated(
        x, w[:, None, :].astype(x.dtype), window_strides=(1,), padding=[(K - 1, 0)],
        dimension_numbers=("NWC", "WIO", "NWC"), feature_group_count=C)
    return out + b.astype(x.dtype)


def gla_chunked(q, k, v, log_a):
    out_dtype = v.dtype
    q, k, v, log_a = (t.astype(jnp.float32) for t in (q, k, v, log_a))
    B, H, T, dk = q.shape
    dv = v.shape[-1]
    n = T // CHUNK

    def to_chunks(t):
        return jnp.moveaxis(t.reshape(B, H, n, CHUNK, t.shape[-1]), 2, 0)

    qc, kc, vc, ac = (to_chunks(t) for t in (q, k, v, log_a))
    mask = jnp.tril(jnp.ones((CHUNK, CHUNK), dtype=bool))[:, :, None]

    def step(S, inp):
        qi, ki, vi, ai = inp
        b = jnp.cumsum(ai, axis=-2)
        b_last = b[..., -1:, :]
        o_inter = jnp.einsum('bhck,bhkv->bhcv', qi * jnp.exp(b), S)
        diff = b[..., :, None, :] - b[..., None, :, :]
        decay = jnp.where(mask, jnp.exp(jnp.where(mask, diff, 0.0)), 0.0)
        scores = jnp.einsum('bhik,bhjk,bhijk->bhij', qi, ki, decay)
        o = o_inter + jnp.einsum('bhij,bhjv->bhiv', scores, vi)
        S_new = S * jnp.exp(b_last)[..., 0, :, None] + jnp.einsum(
            'bhck,bhcv->bhkv', ki * jnp.exp(b_last - b), vi)
        return S_new, o

    S0 = jnp.zeros((B, H, dk, dv), jnp.float32)
    _, oc = lax.scan(step, S0, (qc, kc, vc, ac))
    return jnp.moveaxis(oc, 0, 2).reshape(B, H, T, dv).astype(out_dtype)


def setup_inputs(seed: int = 0) -> dict:
    key = jax.random.key(seed)
    ks = jax.random.split(key, 24)
    nrm = lambda k, shape, s: jax.random.normal(k, shape, jnp.float32) * s
    return {
        "x": nrm(ks[0], (BATCH, SEQ, D_MODEL), 1.0),
        "meta_tokens": nrm(ks[1], (N_META, D_MODEL), 1.0),
        "norm_mix_g": 1.0 + nrm(ks[2], (DEPTH, D_MODEL), 0.02),
        "w_in": nrm(ks[3], (DEPTH, D_MODEL, N_IN), D_MODEL ** -0.5),
        "w_alpha_up": nrm(ks[4], (DEPTH, GLA_RANK, GLA_DK), GLA_RANK ** -0.5),
        "b_alpha": nrm(ks[5], (DEPTH, GLA_DK), 0.1),
        "gla_norm_g": 1.0 + nrm(ks[6], (DEPTH, GLA_DV), 0.02),
        "w_gla_o": nrm(ks[7], (DEPTH, GLA_DV, D_MODEL), GLA_DV ** -0.5),
        "conf_dw_w": nrm(ks[8], (DEPTH, CONF_K, CONF_CH), CONF_K ** -0.5),
        "conf_dw_b": nrm(ks[9], (DEPTH, CONF_CH), 0.02),
        "conf_ln_g": 1.0 + nrm(ks[10], (DEPTH, CONF_CH), 0.02),
        "conf_ln_b": nrm(ks[11], (DEPTH, CONF_CH), 0.02),
        "w_conf_o": nrm(ks[12], (DEPTH, CONF_CH, D_MODEL), CONF_CH ** -0.5),
        "w_out": nrm(ks[13], (DEPTH, D_MODEL, D_MODEL), D_MODEL ** -0.5),
        "norm_ffn_g": 1.0 + nrm(ks[14], (DEPTH, D_MODEL), 0.02),
        "w_up": nrm(ks[15], (DEPTH, D_MODEL, 2 * D_FF), D_MODEL ** -0.5),
        "ffn_dw_w": nrm(ks[16], (DEPTH, FFN_K, D_FF), FFN_K ** -0.5),
        "ffn_dw_b": nrm(ks[17], (DEPTH, D_FF), 0.02),
        "w_down": nrm(ks[18], (DEPTH, D_FF, D_MODEL), D_FF ** -0.5),
        "final_norm_g": 1.0 + nrm(ks[19], (D_MODEL,), 0.02),
    }


def reference(x, meta_tokens, norm_mix_g, w_in, w_alpha_up, b_alpha, gla_norm_g, w_gla_o,
              conf_dw_w, conf_dw_b, conf_ln_g, conf_ln_b, w_conf_o, w_out, norm_ffn_g,
              w_up, ffn_dw_w, ffn_dw_b, w_down, final_norm_g):
    B, S, D = x.shape
    L = S + N_META
    meta = jnp.broadcast_to(meta_tokens[None].astype(x.dtype), (B, N_META, D))
    h = jnp.concatenate([meta, x], axis=1)
    pts = split_points(IN_WIDTHS)
    seq_pad = [(0, 0), (0, 0), (META_PAD, 0), (0, 0)]

    def heads(t):
        return t.reshape(B, L, GLA_HEADS, -1).transpose(0, 2, 1, 3)

    for l in range(DEPTH):
        u = rms_norm(h, norm_mix_g[l])
        proj = u @ w_in[l]
        q, k, v, r, a_lr, c_in, g_gla, g_conf = jnp.split(proj, pts, axis=-1)

        log_a = jax.nn.log_sigmoid(
            (a_lr @ w_alpha_up[l] + b_alpha[l]).astype(jnp.float32)) / GATE_TAU
        qh = jnp.pad(heads(q) * (GLA_DKH ** -0.5), seq_pad)
        kh = jnp.pad(heads(k), seq_pad)
        vh = jnp.pad(heads(v), seq_pad)
        ah = jnp.pad(heads(log_a), seq_pad)
        o = gla_chunked(qh, kh, vh, ah)[:, :, META_PAD:]
        o = rms_norm(o.transpose(0, 2, 1, 3), gla_norm_g[l].reshape(GLA_HEADS, GLA_DVH))
        o = o.reshape(B, L, GLA_DV) * jax.nn.silu(r)
        br_gla = o @ w_gla_o[l]

        c1, c2 = jnp.split(c_in, 2, axis=-1)
        c = c1 * jax.nn.sigmoid(c2)
        c = causal_dwconv(c, conf_dw_w[l], conf_dw_b[l])
        c = layer_norm(c, conf_ln_g[l], conf_ln_b[l])
        br_conf = jax.nn.silu(c) @ w_conf_o[l]

        merged = jax.nn.sigmoid(g_gla) * br_gla + jax.nn.sigmoid(g_conf) * br_conf
        h = h + merged @ w_out[l]

        u = rms_norm(h, norm_ffn_g[l])
        a, bv = jnp.split(u @ w_up[l], 2, axis=-1)
        a = causal_dwconv(a, ffn_dw_w[l], ffn_dw_b[l])
        h = h + (jax.nn.silu(a) * bv) @ w_down[l]

    h = rms_norm(h, final_norm_g)
    return h[:, N_META:]
```

```python
import numpy as np
from contextlib import ExitStack
import concourse.bass as bass
import concourse.mybir as mybir
from concourse.bass_utils import run_bass_kernel_spmd

F32 = mybir.dt.float32
BF16 = mybir.dt.bfloat16
AF = mybir.ActivationFunctionType
ALU = mybir.AluOpType

D = 1024
NB = 3
TTM = NB * 128
DFF = 2816
NFC = DFF // 128
CK = 31
HALO = CK - 1
N_IN = 7184
RMS_EPS = 1e-6
LN_EPS = 1e-5
NSLOT = 5
import os as _os
NPE_TAPS = int(_os.environ.get('K_TAPS', '29'))
K_PRE = int(_os.environ.get('K_PRE', '71'))
K_CASTINT = int(_os.environ.get('K_CASTINT', '1'))
K_PEATT = int(_os.environ.get('K_PEATT', '0'))
NPSUM = 8

R_FG, R_G1, R_GG, R_CB, R_LG, R_LB, R_G2, R_CW, R_BA, R_FF = 0, 1, 2, 3, 4, 5, 6, 7, 38, 39
NROW = 64


class Prog:
    CE = ('pe', 'act', 'dve', 'pool')
    K = 8
    FINAL = 1 << 40

    def __init__(self, nc, stack):
        self.nc = nc
        self.stack = stack
        self.ops = {e: [] for e in ('pe', 'act', 'dve', 'pool', 'sp')}
        self.sems = {e: [stack.enter_context(nc.semaphore(f"s_{e}{i}")) for i in range(self.K)]
                     for e in self.CE}
        self.cnt = {e: 0 for e in self.CE}
        self.seen = {e: {} for e in self.ops}
        self.res = {}
        self.dsem = {}
        self.out_events = []

    def _wait_for(self, eng, ev, waits):
        kind, src, val = ev
        key = (kind, src)
        if self.seen[eng].get(key, -1) >= val:
            return
        self.seen[eng][key] = val
        if kind == 'eng':
            waits.append((self.sems[src][val % self.K], val // self.K + 1))
        elif val == self.FINAL:
            waits.append((self.dsem[src], None))
        else:
            waits.append((self.dsem[src][0], val))

    def _deps(self, eng, r, w):
        waits = []
        evs = []
        for k in r:
            st = self.res.get(k)
            if st and st[0] is not None:
                evs.append((st[0], True))
        for k in w:
            st = self.res.get(k)
            if st:
                if st[0] is not None:
                    evs.append((st[0], False))
                for ev in st[1].values():
                    evs.append((ev, False))
        for ev, raw in evs:
            if ev[0] == 'eng' and ev[1] == eng and eng == 'pe':
                continue
            self._wait_for(eng, ev, waits)
        return waits

    def _commit(self, ev, r, w):
        for k in r:
            st = self.res.setdefault(k, [None, {}])
            st[1][(ev[0], ev[1])] = ev
        for k in w:
            self.res[k] = [ev, {}]

    def op(self, eng, emit, r=(), w=()):
        waits = self._deps(eng, r, w)
        n = self.cnt[eng]
        self.cnt[eng] = n + 1
        ev = ('eng', eng, n)
        self.ops[eng].append((emit, waits, (self.sems[eng][n % self.K], 1), False))
        self._commit(ev, r, w)

    def dma(self, eng, out, in_, key, r=(), w=(), is_out=False, group=False):
        if key not in self.dsem:
            self.dsem[key] = [self.stack.enter_context(self.nc.semaphore(f"d_{len(self.dsem)}")), 0]
        waits = self._deps(eng, r, w)
        self.dsem[key][1] += 16
        val = self.FINAL if group else self.dsem[key][1]
        ev = ('dma', key, val)
        self.ops[eng].append((lambda e: e.dma_start(out=out, in_=in_), waits, (self.dsem[key][0], 16), True))
        self._commit(ev, r, w)
        if is_out:
            self.out_events.append(ev)

    def emit(self):
        nc = self.nc
        fin = []
        for ev in self.out_events:
            self._wait_for('sp', ev, fin)
        with nc.Block() as block:
            def run(e, name, final=None):
                attach = name in (('pe', 'act', 'dve', 'pool') if K_PEATT else ('act', 'dve', 'pool'))
                for emit, waits, inc, isdma in self.ops[name]:
                    ws = [(s_[0], s_[1]) if v is None else (s_, v) for s_, v in waits]
                    first = ws.pop(0) if (attach and ws and not isdma) else None
                    for s, v in ws:
                        e.wait_ge(s, v)
                    ins = emit(e)
                    if first is not None:
                        ins._wait_ge(first[0], first[1])
                    ins.then_inc(inc[0], inc[1])
                if final:
                    for s_, v in final:
                        if v is None:
                            s_, v = s_[0], s_[1]
                        e.wait_ge(s_, v)

            @block.tensor
            def _(e):
                run(e, 'pe')

            @block.scalar
            def _(e):
                run(e, 'act')

            @block.vector
            def _(e):
                run(e, 'dve')

            @block.gpsimd
            def _(e):
                run(e, 'pool')

            @block.sync
            def _(e):
                run(e, 'sp', fin)


def build_program(npre_blocks, nfull_blocks):
    nc = bass.Bass("TRN2", target_bir_lowering=False)
    dt = lambda name, shape, dtype, kind: nc.dram_tensor(name, shape, dtype, kind=kind).ap()
    xp = dt("xp", [max(npre_blocks, 1) * 128, D], F32, "ExternalInput")
    xf = dt("xf", [nfull_blocks * 128, D], F32, "ExternalInput")
    w_in = dt("w_in", [D, N_IN], F32, "ExternalInput")
    w_au = dt("w_alpha_up", [16, 512], F32, "ExternalInput")
    w_go = dt("w_gla_o", [D, D], F32, "ExternalInput")
    w_co = dt("w_conf_o", [D, D], F32, "ExternalInput")
    w_o = dt("w_out", [D, D], F32, "ExternalInput")
    w_up = dt("w_up", [D, 2 * DFF], F32, "ExternalInput")
    w_dn = dt("w_down", [DFF, D], F32, "ExternalInput")
    vin = {}
    for nm, n in (("final_norm_g", D), ("norm_mix_g", D), ("gla_norm_g", D), ("conf_dw_b", D),
                  ("conf_ln_g", D), ("conf_ln_b", D), ("norm_ffn_g", D), ("b_alpha", 512),
                  ("ffn_dw_b", DFF)):
        vin[nm] = dt(nm, [1, n], F32, "ExternalInput")
    cdw = dt("conf_dw_w", [CK, D], F32, "ExternalInput")
    fdw = dt("ffn_dw_w", [3, DFF], F32, "ExternalInput")
    identf_d = dt("identf", [128, 128], F32, "ExternalInput")
    mask_d = dt("cmask", [128, 128], F32, "ExternalInput")
    y = dt("y", [nfull_blocks * 128, D], F32, "ExternalOutput")

    pieces = []

    def ws_piece(W, c0, ncols=256):
        return (W[:, c0:c0 + ncols].rearrange("(k p) j -> p k j", p=128), 8, ncols)

    def as_piece(W, k0, nk, c0):
        return (W[k0 * 128:(k0 + nk) * 128, c0:c0 + 512].rearrange("(k p) j -> p k j", p=128), nk, 512)

    pid = {}

    def add(name, spec):
        pid[name] = len(pieces)
        pieces.append(spec)

    add('alr', ws_piece(w_in, 3072, 16))
    for i in range(2):
        add(('k', i), ws_piece(w_in, 512 + 256 * i))
    for g in range(2):
        for kh in range(2):
            add(('v', g, kh), as_piece(w_in, 4 * kh, 4, 1024 + 512 * g))
    for i in range(2):
        add(('q', i), ws_piece(w_in, 256 * i))
    for g in range(2):
        for kh in range(2):
            add(('r', g, kh), as_piece(w_in, 4 * kh, 4, 2048 + 512 * g))
    for i in range(4):
        add(('c2', i), ws_piece(w_in, 4112 + 256 * i))
        add(('c1', i), ws_piece(w_in, 3088 + 256 * i))
    for i in range(4):
        add(('gg', i), ws_piece(w_in, 5136 + 256 * i))
    for i in range(4):
        add(('gc', i), ws_piece(w_in, 6160 + 256 * i))
    for i in range(4):
        add(('go', i), ws_piece(w_go, 256 * i))
    for i in range(4):
        add(('co', i), ws_piece(w_co, 256 * i))
    for g in range(2):
        for kh in range(2):
            add(('wo', g, kh), as_piece(w_o, 4 * kh, 4, 512 * g))
    for i in range(11):
        add(('ua', i), ws_piece(w_up, 256 * i))
        add(('ub', i), ws_piece(w_up, DFF + 256 * i))
    for g in range(2):
        for kp in range(6):
            nk = 4 if kp < 5 else 2
            add(('dn', g, kp), as_piece(w_dn, 4 * kp, nk, 512 * g))
    NP = len(pieces)
    wsc = nc.dram_tensor("wsc", [NP, 128, 2048], BF16, kind="Internal").ap()

    with ExitStack() as stack:
        P = Prog(nc, stack)
        sb = lambda name, shape, dtype: stack.enter_context(nc.sbuf_tensor(name, shape, dtype))
        hb = [sb(f"hb{i}", [128, NB, D], F32) for i in range(2)]
        xn = sb("xn", [128, NB, D], BF16)
        Fm0 = sb("Fm0", [128, 8, 2 + TTM], BF16)
        Fm = [Fm0[:, :, 2:2 + TTM], sb("Fm1", [128, 8, TTM], BF16)]
        u2h = sb("u2h", [128, 8, 2], BF16)
        qt = sb("qt", [128, 4, TTM], BF16)
        kt = sb("kt", [128, 4, TTM], BF16)
        alr_sb = sb("alr_sb", [16, TTM], BF16)
        a_e = sb("a_e", [128, 3, TTM], F32)
        a_B = sb("a_B", [128, 2, TTM], F32)
        ebt = sb("ebt", [128, 4, TTM], F32)
        enbt = sb("enbt", [128, 4, TTM], F32)
        vt = sb("vt", [128, NB, D], BF16)
        rs = sb("rs", [128, NB, D], BF16)
        cbuf = sb("cbuf", [128, 8, HALO + TTM], BF16)
        sg1 = sb("sg1", [128, 8, TTM], BF16)
        sg2 = sb("sg2", [128, 8, TTM], BF16)
        S32 = sb("S32", [128, 4, 256], F32)
        Sdec = sb("Sdec", [128, 2, 256], F32)
        Sbf = sb("Sbf", [128, 4, 256], BF16)
        kTs = sb("kTs", [128, 2, 512], BF16)
        sTs = sb("sTs", [128, 2, 512], BF16)
        junk = sb("junk", [128, 256], BF16)
        mtmp = sb("mtmp", [128, 2, TTM], F32)
        ybf = sb("ybf", [128, 2, TTM], BF16)
        ysq = sb("ysq", [128, 2, TTM], BF16)
        diagR = sb("diagR", [128, 8 * NPE_TAPS, 128], BF16)
        colv = sb("colv", [128, 8, NROW], F32)
        negb = sb("negb", [128, 4], F32)
        fgb = sb("fgb", [128, D], F32)
        identb = sb("identb", [128, 128], BF16)
        identf = sb("identf_s", [128, 128], F32)
        maskt = sb("maskt", [128, 128], F32)
        mask4 = sb("mask4", [128, 512], BF16)
        onesb = sb("onesb", [128, 128], BF16)
        wau = sb("wau", [16, 512], BF16)
        rmask = sb("rmask", [128, TTM], F32)
        st_ss = sb("st_ss", [128, 16], F32)
        st_l = sb("st_l", [128, 16], F32)
        st_r = sb("st_r", [128, 16], F32)
        wsl = [sb(f"wsl{i}", [128, 2048], BF16) for i in range(NSLOT)]
        psb = [stack.enter_context(nc.psum_tensor(f"ps{i}", [128, 512], F32)) for i in range(NPSUM)]

        def actT_c(m):
            if m < 8:
                return sg1[:, m, :], ('sg1', m)
            if m < 16:
                return sg2[:, m - 8, :], ('sg2', m - 8)
            if m < 20:
                return qt[:, m - 16, :], ('qt', m - 16)
            return kt[:, m - 20, :], ('kt', m - 20)

        def y32_c(c):
            return (ebt[:, c, :], ('eb', c)) if c < 4 else (enbt[:, c - 4, :], ('enb', c - 4))

        def fm_c(buf, key):
            return lambda kc: (buf[:, kc, :], (key, kc))

        bank_state = {'next': 0, 'pinned': set()}

        def bank(pin=False):
            for _ in range(NPSUM):
                b = bank_state['next']
                bank_state['next'] = (b + 1) % NPSUM
                if b not in bank_state['pinned']:
                    if pin:
                        bank_state['pinned'].add(b)
                    return b
            raise RuntimeError("all PSUM banks pinned")

        def unpin(b):
            bank_state['pinned'].discard(b)

        PK = lambda b: ('ps', b)

        stream = {'pos': 0, 'issued': 0, 'order': []}

        def issue_to(n):
            while stream['issued'] < min(n, len(stream['order'])):
                i = stream['issued']
                p = stream['order'][i]
                s = i % NSLOT
                n_ = pieces[p][1] * pieces[p][2]
                assert ('wsc', p) in P.res, "stream DMA recorded before its cast DMA"
                P.dma('sp', wsl[s][:, 0:n_], wsc[p][:, 0:n_], key=('wsl', s), r=[('wsc', p)], w=[('wsl', s)])
                stream['issued'] += 1

        def piece(name):
            i = stream['pos']
            assert stream['order'][i] == pid[name], (name, i)
            issue_to(i + NSLOT)
            stream['pos'] += 1
            s = i % NSLOT
            _, k, j = pieces[pid[name]]
            return wsl[s][:, 0:k * j].rearrange("p (k j) -> p k j", k=k), ('wsl', s)

        P.dma('pool', wau[:, :], w_au[:, :], key='cst_wau', w=['wau'])
        cast_groups = {}
        for p_i, (src, k, j) in enumerate(pieces):
            dst = wsc[p_i][:, 0:k * j].rearrange("p (k j) -> p k j", k=k)
            gi = 0 if p_i < 7 else 1 + (p_i - 7) // 9
            cast_groups.setdefault(gi, []).append((p_i, dst, src))

        def emit_cast_group(gi):
            for p_i, dst, src in cast_groups.pop(gi, []):
                P.dma('pool', dst, src, key=('cast', gi), w=[('wsc', p_i)], group=True)

        emit_cast_group(0)
        if not K_CASTINT:
            for gi in sorted(cast_groups):
                emit_cast_group(gi)
        P.dma('sp', identf[:, :], identf_d[:, :], key='cst', w=['identf'], group=True)
        P.dma('sp', maskt[:, :], mask_d[:, :], key='cst', w=['mask'], group=True)
        P.dma('sp', fgb[:, :], vin["final_norm_g"].partition_broadcast(128), key='cst', w=['fgb'], group=True)
        vecs = hb[1]
        VK = [('h', 1, b) for b in range(NB)]
        vrows = []
        for r_, nm in [(R_G1, "norm_mix_g"), (R_GG, "gla_norm_g"), (R_CB, "conf_dw_b"), (R_LG, "conf_ln_g"),
                       (R_LB, "conf_ln_b"), (R_G2, "norm_ffn_g")]:
            vrows.append((r_, 1, D, vin[nm][:, :]))
        vrows.append((R_CW, CK, D, cdw[:, :]))
        vrows.append((R_BA, 1, 512, vin["b_alpha"][:, :]))
        for v_ in range(4):
            src = fdw[v_:v_ + 1, :] if v_ < 3 else vin["ffn_dw_b"][:, :]
            for s_ in range(3):
                n_ = 1024 if s_ < 2 else DFF - 2048
                vrows.append((R_FF + v_ * 3 + s_, 1, n_, src[:, s_ * 1024:s_ * 1024 + n_]))
        VRK = [('vecs', i) for i in range(len(vrows))]
        P.op('dve', lambda e: e.memset(vecs[0:NROW, 0, :], 0.0), w=VK + VRK)
        for i, (r_, nr, n_, src) in enumerate(vrows):
            P.dma('sp', vecs[r_:r_ + nr, 0, 0:n_], src, key='cst', w=[('vecs', i)], group=True)
        VK = VK + VRK
        P.op('act', lambda e: e.activation(out=identb[:, :], in_=identf[:, :], func=AF.Copy),
             r=['identf'], w=['identb'])
        P.op('dve', lambda e: e.memset(onesb[:, :], 1.0 / D), w=['onesb'])
        for h in range(4):
            P.op('dve', lambda e, h=h: e.tensor_copy(mask4[:, h * 128:(h + 1) * 128], maskt[:, :]), r=['mask'], w=['mask4'])
        P.op('dve', lambda e: e.memset(rmask[:, :], 1.0), w=['rmask'])
        for b in range(NB):
            P.op('dve', lambda e, b=b: e.memset(rmask[:, b * 128:b * 128 + 1], 0.0), w=['rmask'])
        P.op('dve', lambda e: e.memset(cbuf[:, :, 0:HALO], 0.0), w=[('cbuf', c) for c in range(8)])
        P.op('dve', lambda e: e.memset(u2h[:, :, :], 0.0), w=['u2h'])
        P.op('dve', lambda e: e.memset(S32[:, :, :], 0.0), w=[('S32', h) for h in range(4)])
        P.op('dve', lambda e: e.memset(Sbf[:, :, :], 0.0), w=[('Sbf', h) for h in range(4)])
        for c in range(8):
            b_ = bank()
            P.op('pe', lambda e, b_=b_, c=c: e.transpose(psb[b_][:, 0:NROW], vecs[0:NROW, 0, c * 128:(c + 1) * 128],
                                                        identf[0:NROW, 0:NROW]),
                 r=VK + ['identf'], w=[PK(b_)])
            P.op('dve', lambda e, b_=b_, c=c: e.tensor_copy(colv[:, c, :], psb[b_][:, 0:NROW]),
                 r=[PK(b_)], w=['colv'])
        P.op('dve', lambda e: e.tensor_scalar(negb[:, :], colv[:, 0:4, R_BA], -1.0, None, ALU.mult),
             r=['colv'], w=['negb'])

        def cv(c, r_):
            return colv[:, c, r_:r_ + 1]

        diag_todo = [(c, j) for c in range(8) for j in range(NPE_TAPS)]

        def emit_diag(n):
            for _ in range(min(n, len(diag_todo))):
                c, j = diag_todo.pop(0)
                P.op('pool', lambda e, c=c, j=j: e.tensor_scalar(
                    diagR[:, c * NPE_TAPS + j, :], identb[:, :], cv(c, R_CW + j), 0.0, ALU.mult, ALU.add),
                     r=['identb', 'colv'], w=['diagR'])

        def ffv(m, v_):
            s_, c = divmod(m, 8)
            return colv[:, c, R_FF + v_ * 3 + s_:R_FF + v_ * 3 + s_ + 1]

        tilectr = {'n': 0}

        def rstd_chain(ncols, dim, eps):
            P.op('act', lambda e: e.activation(out=st_l[:, 0:ncols], in_=st_ss[:, 0:ncols], func=AF.Ln,
                                               scale=1.0 / dim, bias=eps), r=['st_ss'], w=['st_l'])
            P.op('act', lambda e: e.activation(out=st_r[:, 0:ncols], in_=st_l[:, 0:ncols], func=AF.Exp,
                                               scale=-0.5), r=['st_l'], w=['st_r'])

        def norm_transpose(hbuf, slot, nb, grow, dst, dkey, pre=False):
            for b in range(nb):
                P.op('act', lambda e, b=b: e.activation(out=xn[:, b, :], in_=hbuf[:, b, :], func=AF.Square,
                                                        accum_out=st_ss[:, b:b + 1]),
                     r=[('h', slot, b)], w=[('xn', b), 'st_ss'])
            rstd_chain(nb, D, RMS_EPS)
            for b in range(nb):
                if (pre and (K_PRE & 1)) or (not pre and (K_PRE & 64)):
                    P.op('dve', lambda e, b=b: e.tensor_scalar(xn[:, b, :], hbuf[:, b, :], st_r[:, b:b + 1], None,
                                                               ALU.mult),
                         r=[('h', slot, b), 'st_r'], w=[('xn', b)])
                else:
                    P.op('act', lambda e, b=b: e.activation(out=xn[:, b, :], in_=hbuf[:, b, :], func=AF.Copy,
                                                            scale=st_r[:, b:b + 1]),
                         r=[('h', slot, b), 'st_r'], w=[('xn', b)])
            if dst is not None:
                transpose_block_set(xn, 'xn', nb, grow, dst, dkey, pre)

        def transpose_block_set(src, skey, nb, grow, dst, dkey, pre=False):
            for b in range(nb):
                b_ = bank()
                pst = psb[b_][:, :].bitcast(BF16)
                for c in range(8):
                    P.op('pe', lambda e, b=b, c=c, pst=pst: e.transpose(pst[:, c * 128:(c + 1) * 128],
                                                                       src[:, b, c * 128:(c + 1) * 128],
                                                                       identb[:, :]),
                         r=[(skey, b), 'identb'], w=[PK(b_)])
                for c in range(8):
                    if pre and (K_PRE & 16) and c % 2 == 1:
                        P.op('dve', lambda e, b=b, c=c, pst=pst: e.tensor_scalar(
                            dst[:, c, b * 128:(b + 1) * 128], pst[:, c * 128:(c + 1) * 128], cv(c, grow), None,
                            ALU.mult),
                             r=[PK(b_), 'colv'], w=[(dkey, c)])
                    else:
                        P.op('act', lambda e, b=b, c=c, pst=pst: e.activation(
                            out=dst[:, c, b * 128:(b + 1) * 128], in_=pst[:, c * 128:(c + 1) * 128],
                            func=AF.Copy, scale=cv(c, grow)),
                             r=[PK(b_), 'colv'], w=[(dkey, c)])

        def ws_matmul(wp, wkey, jc, src, skey, TT, b_):
            for k in range(8):
                P.op('pe', lambda e, k=k: e.matmul(psb[b_][:, 0:TT], wp[:, k, jc * 128:(jc + 1) * 128],
                                                  src[:, k, 0:TT], start=(k == 0), stop=(k == 7)),
                     r=[wkey, (skey, k)], w=[PK(b_)])

        def as_group(names, srcf, nb, nkc_list, evac):
            banks = [bank(pin=True) for _ in range(nb)]
            kbase = 0
            npieces = len(names)
            for pi, nm in enumerate(names):
                wp, wkey = piece(nm)
                nk = nkc_list[pi]
                for b in range(nb):
                    for kk in range(nk):
                        kc = kbase + kk
                        st_ = (kc == 0)
                        sp_ = (pi == npieces - 1 and kk == nk - 1)
                        sap, skey_ = srcf(kc)
                        P.op('pe', lambda e, b=b, kk=kk, sap=sap, wp=wp, st_=st_, sp_=sp_, bk=banks[b]: e.matmul(
                            psb[bk][:, 0:512], sap[:, b * 128:(b + 1) * 128], wp[:, kk, :],
                            start=st_, stop=sp_),
                             r=[wkey, skey_], w=[PK(banks[b])])
                kbase += nk
            for b in range(nb):
                evac(b, banks[b])
                unpin(banks[b])

        def alpha_and_k(uT, TT, nb, full, fk='F0'):
            wp, wkey = piece('alr')
            b_ = bank()
            for k in range(8):
                P.op('pe', lambda e, k=k, b_=b_, wp=wp: e.matmul(psb[b_][0:16, 0:TT], wp[:, k, 0:16], uT[:, k, 0:TT],
                                                              start=(k == 0), stop=(k == 7)),
                     r=[wkey, (fk, k)], w=[PK(b_)])
            P.op('act', lambda e, b_=b_: e.activation(out=alr_sb[:, 0:TT], in_=psb[b_][0:16, 0:TT], func=AF.Copy),
                 r=[PK(b_)], w=['alr'])
            for h in range(4):
                bz = bank()
                P.op('pe', lambda e, h=h, bz=bz: e.matmul(psb[bz][:, 0:TT], wau[:, h * 128:(h + 1) * 128],
                                                        alr_sb[:, 0:TT], start=True, stop=True),
                     r=['wau', 'alr'], w=[PK(bz)])
                P.op('act', lambda e, h=h, bz=bz: e.activation(out=enbt[:, h, 0:TT], in_=psb[bz][:, 0:TT],
                                                             func=AF.Exp, scale=-1.0, bias=negb[:, h:h + 1]),
                     r=[PK(bz), 'negb'], w=[('enb', h)])
            for h in range(4):
                P.op('act', lambda e, h=h: e.activation(out=enbt[:, h, 0:TT], in_=enbt[:, h, 0:TT], func=AF.Ln,
                                                       bias=1.0), r=[('enb', h)], w=[('enb', h)])
            for h in range(4):
                P.op('dve', lambda e, h=h: e.tensor_tensor_scan(a_B[:, h % 2, 0:TT], rmask[:, 0:TT],
                                                               enbt[:, h, 0:TT], 0.0, ALU.mult, ALU.add),
                     r=[('enb', h), 'rmask'], w=[('a_B', h % 2)])
                if full:
                    P.op('act', lambda e, h=h: e.activation(out=ebt[:, h, 0:TT], in_=a_B[:, h % 2, 0:TT],
                                                           func=AF.Exp, scale=-1.0 / 16), r=[('a_B', h % 2)],
                         w=[('eb', h)])
                else:
                    for b in range(nb):
                        c_ = b * 128 + 127
                        P.op('act', lambda e, h=h, c_=c_: e.activation(out=ebt[:, h, c_:c_ + 1],
                                                                     in_=a_B[:, h % 2, c_:c_ + 1],
                                                                     func=AF.Exp, scale=-1.0 / 16),
                             r=[('a_B', h % 2)], w=[('eb', h)])
                P.op('act', lambda e, h=h: e.activation(out=enbt[:, h, 0:TT], in_=a_B[:, h % 2, 0:TT],
                                                       func=AF.Exp, scale=1.0 / 16), r=[('a_B', h % 2)],
                     w=[('enb', h)])

        def k_proj(uT, TT, fk='F0'):
            for i in range(2):
                wp, wkey = piece(('k', i))
                for jc in range(2):
                    h = 2 * i + jc
                    b_ = bank()
                    ws_matmul(wp, wkey, jc, uT, fk, TT, b_)
                    P.op('dve', lambda e, h=h, b_=b_: e.tensor_tensor(kt[:, h, 0:TT], psb[b_][:, 0:TT],
                                                                    enbt[:, h, 0:TT], ALU.mult),
                         r=[PK(b_), ('enb', h)], w=[('kt', h)])

        def v_proj(uT, nb, pre=False, fk='F0'):
            for g in range(2):
                def evac(b, bk, g=g):
                    if pre and (K_PRE & 2):
                        P.op('dve', lambda e: e.tensor_copy(vt[:, b, g * 512:(g + 1) * 512], psb[bk][:, 0:512]),
                             r=[PK(bk)], w=[('vt', b)])
                    else:
                        P.op('act', lambda e: e.activation(out=vt[:, b, g * 512:(g + 1) * 512],
                                                           in_=psb[bk][:, 0:512], func=AF.Copy),
                             r=[PK(bk)], w=[('vt', b)])
                as_group([('v', g, 0), ('v', g, 1)], fm_c(uT, fk), nb, [4, 4], evac)

        def kT_block(b):
            sl = slice(b * 128, (b + 1) * 128)
            bt = bank()
            pst = psb[bt][:, :].bitcast(BF16)
            for h in range(4):
                P.op('pe', lambda e, h=h: e.transpose(pst[:, h * 128:(h + 1) * 128], kt[:, h, sl], identb[:, :]),
                     r=[('kt', h), 'identb'], w=[PK(bt)])
            P.op('act', lambda e: e.activation(out=kTs[:, b % 2, :], in_=pst[:, 0:512], func=AF.Copy),
                 r=[PK(bt)], w=[('kTs', b % 2)])

        def state_block(b, pre=False):
            for hp in range(2):
                bd = bank()
                for hh in range(2):
                    h = 2 * hp + hh
                    P.op('pe', lambda e, h=h, hh=hh, bd=bd: e.matmul(
                        psb[bd][:, hh * 256:(hh + 1) * 256], kTs[:, b % 2, h * 128:(h + 1) * 128],
                        vt[:, b, h * 256:(h + 1) * 256], start=True, stop=True),
                         r=[('kTs', b % 2), ('vt', b)], w=[PK(bd)])
                for hh in range(2):
                    h = 2 * hp + hh
                    el = ebt[:, h, b * 128 + 127:b * 128 + 128]
                    ks = h % 2
                    P.op('pool', lambda e, h=h, el=el, ks=ks: e.tensor_scalar(
                        Sdec[:, ks, :], S32[:, h, :], el, 0.0, ALU.mult, ALU.add),
                         r=[('S32', h), ('eb', h)], w=[('Sdec', ks)])
                    P.op('dve', lambda e, h=h, hh=hh, el=el, ks=ks, bd=bd: e.scalar_tensor_tensor(
                        S32[:, h, :], psb[bd][:, hh * 256:(hh + 1) * 256], el, Sdec[:, ks, :], ALU.mult, ALU.add),
                         r=[PK(bd), ('Sdec', ks), ('eb', h)], w=[('S32', h)])
                    if pre and (K_PRE & 4):
                        P.op('dve', lambda e, h=h: e.tensor_copy(Sbf[:, h, :], S32[:, h, :]),
                             r=[('S32', h)], w=[('Sbf', h)])
                    else:
                        P.op('act', lambda e, h=h: e.activation(out=Sbf[:, h, :], in_=S32[:, h, :], func=AF.Copy),
                             r=[('S32', h)], w=[('Sbf', h)])

        def load_tile(src, row0, nb):
            slot = tilectr['n'] % 2
            tilectr['n'] += 1
            P.dma('sp', hb[slot][:, 0:nb, :], src[row0:row0 + nb * 128, :].rearrange("(b p) d -> p b d", p=128),
                  key=('hld', slot), w=[('h', slot, b) for b in range(nb)])
            return slot

        order = []
        pre_tiles = []
        r0 = 0
        while r0 < npre_blocks:
            nb = min(NB, npre_blocks - r0)
            pre_tiles.append((r0, nb))
            r0 += nb
        pre_names = ['alr'] + [('v', g, kh) for g in range(2) for kh in range(2)] + [('k', 0), ('k', 1)]
        full_names = (['alr'] + [('v', g, kh) for g in range(2) for kh in range(2)]
                      + [('r', g, kh) for g in range(2) for kh in range(2)]
                      + [x for i in range(4) for x in (('c2', i), ('c1', i))]
                      + [('gg', i) for i in range(4)] + [('gc', i) for i in range(4)]
                      + [('k', 0), ('k', 1), ('q', 0), ('q', 1)]
                      + [('go', i) for i in range(4)] + [('co', i) for i in range(4)]
                      + [('wo', g, kh) for g in range(2) for kh in range(2)]
                      + [x for i in range(11) for x in (('ua', i), ('ub', i))]
                      + [('dn', g, kp) for g in range(2) for kp in range(6)])
        assert nfull_blocks % NB == 0
        nfull_tiles = nfull_blocks // NB
        for _ in pre_tiles:
            order += [pid[n] for n in pre_names]
        for _ in range(nfull_tiles):
            order += [pid[n] for n in full_names]
        stream['order'] = order

        pre_slot = {}
        full_slot = {}

        def pre_load(i_):
            if i_ < len(pre_tiles):
                blk0, nb = pre_tiles[i_]
                pre_slot[i_] = load_tile(xp, blk0 * 128, nb)
            elif i_ == len(pre_tiles):
                full_slot[0] = load_tile(xf, 0, NB)

        def pre_norm(i_):
            blk0, nb = pre_tiles[i_]
            slot = pre_slot[i_]
            norm_transpose(hb[slot], slot, nb, R_G1, None, None, bool(K_PRE))

        def pre_tr(i_):
            blk0, nb = pre_tiles[i_]
            transpose_block_set(xn, 'xn', nb, R_G1, Fm[i_ % 2], 'F%d' % (i_ % 2), bool(K_PRE))

        def pre_head(i_):
            pre_norm(i_)
            pre_tr(i_)

        def pre_body(i_):
            blk0, nb = pre_tiles[i_]
            TT = nb * 128
            uT_, fk_ = Fm[i_ % 2], 'F%d' % (i_ % 2)
            pre_load(i_ + 2)
            alpha_and_k(uT_, TT, nb, False, fk_)
            if i_ + 1 < len(pre_tiles):
                pre_norm(i_ + 1)
            v_proj(uT_, nb, bool(K_PRE), fk_)
            if i_ + 1 < len(pre_tiles):
                pre_tr(i_ + 1)
            k_proj(uT_, TT, fk_)
            for b in range(nb):
                kT_block(b)
                state_block(b, True)

        pre_load(0)
        if pre_tiles:
            pre_load(1)
            pre_head(0)
        for i_ in range(len(pre_tiles)):
            pre_body(i_)
            if i_ in (1, 4, 7):
                emit_cast_group({1: 1, 4: 2, 7: 3}[i_])
            emit_diag(24)
        for gi in sorted(cast_groups):
            if gi <= 4 or len(pre_tiles) < 10:
                emit_cast_group(gi)
        emit_diag(10 ** 6)

        def tile_head_norm(ti):
            slot = full_slot[ti]
            norm_transpose(hb[slot], slot, NB, R_G1, None, None)
            return slot

        def tile_head_tr():
            transpose_block_set(xn, 'xn', NB, R_G1, Fm[0], 'F0')

        next_slot = {}
        pending_store = []

        def full_tile(ti):
            nb = NB
            TT = TTM
            if ti == 0:
                next_slot[0] = tile_head_norm(0)
                tile_head_tr()
            slot = next_slot[ti]
            H = hb[slot]
            HK = lambda b: ('h', slot, b)
            uT = Fm[0]
            if ti == 0:
                alpha_and_k(uT, TT, nb, True)
            v_proj(uT, nb)
            for g in range(2):
                def evac(b, bk, g=g):
                    P.op('act', lambda e: e.activation(out=rs[:, b, g * 512:(g + 1) * 512], in_=psb[bk][:, 0:512],
                                                       func=AF.Silu), r=[PK(bk)], w=[('rs', b)])
                as_group([('r', g, 0), ('r', g, 1)], fm_c(uT, 'F0'), nb, [4, 4], evac)
            for i in range(4):
                wp, wkey = piece(('c2', i))
                for jc in range(2):
                    b_ = bank()
                    ws_matmul(wp, wkey, jc, uT, 'F0', TT, b_)
                    P.op('act', lambda e, jc=jc, b_=b_: e.activation(out=a_B[:, jc, 0:TT], in_=psb[b_][:, 0:TT],
                                                                   func=AF.Sigmoid), r=[PK(b_)], w=[('a_B', jc)])
                wp, wkey = piece(('c1', i))
                for jc in range(2):
                    c = 2 * i + jc
                    b_ = bank()
                    ws_matmul(wp, wkey, jc, uT, 'F0', TT, b_)
                    P.op('dve', lambda e, jc=jc, c=c, b_=b_: e.tensor_tensor(
                        cbuf[:, c, HALO:HALO + TT], psb[b_][:, 0:TT], a_B[:, jc, 0:TT], ALU.mult),
                         r=[PK(b_), ('a_B', jc)], w=[('cbuf', c)])
            for nm, sg, sk in (('gg', sg1, 'sg1'), ('gc', sg2, 'sg2')):
                for i in range(4):
                    wp, wkey = piece((nm, i))
                    for jc in range(2):
                        c = 2 * i + jc
                        b_ = bank()
                        ws_matmul(wp, wkey, jc, uT, 'F0', TT, b_)
                        P.op('act', lambda e, c=c, b_=b_, sg=sg: e.activation(
                            out=sg[:, c, 0:TT], in_=psb[b_][:, 0:TT], func=AF.Sigmoid),
                             r=[PK(b_)], w=[(sk, c)])
            while pending_store:
                pending_store.pop(0)()
            if ti + 1 < nfull_tiles:
                full_slot[ti + 1] = load_tile(xf, (ti + 1) * TTM, NB)
            k_proj(uT, TT)
            for i in range(2):
                wp, wkey = piece(('q', i))
                for jc in range(2):
                    h = 2 * i + jc
                    b_ = bank()
                    ws_matmul(wp, wkey, jc, uT, 'F0', TT, b_)
                    P.op('dve', lambda e, h=h, b_=b_: e.scalar_tensor_tensor(
                        qt[:, h, 0:TT], psb[b_][:, 0:TT], 128.0 ** -0.5, ebt[:, h, 0:TT], ALU.mult, ALU.mult),
                         r=[PK(b_), ('eb', h)], w=[('qt', h)])
            def gla_scores(b):
                sl = slice(b * 128, (b + 1) * 128)
                bs = bank()
                for h in range(4):
                    P.op('pe', lambda e, h=h: e.matmul(psb[bs][:, h * 128:(h + 1) * 128], kt[:, h, sl], qt[:, h, sl],
                                                      start=True, stop=True),
                         r=[('kt', h), ('qt', h)], w=[PK(bs)])
                P.op('dve', lambda e: e.tensor_tensor(sTs[:, b % 2, :], psb[bs][:, 0:512], mask4[:, :], ALU.mult),
                     r=[PK(bs), 'mask4'], w=[('sTs', b % 2)])

            def gla_out(b):
                sl = slice(b * 128, (b + 1) * 128)
                obanks = [bank(pin=True), bank(pin=True)]
                for h in range(4):
                    bo = obanks[h // 2]
                    oc = slice((h % 2) * 256, (h % 2 + 1) * 256)
                    P.op('pe', lambda e, h=h, bo=bo, oc=oc: e.matmul(psb[bo][:, oc], qt[:, h, sl], Sbf[:, h, :],
                                                                   start=True, stop=False),
                         r=[('qt', h), ('Sbf', h)], w=[PK(bo)])
                    P.op('pe', lambda e, h=h, bo=bo, oc=oc: e.matmul(
                        psb[bo][:, oc], sTs[:, b % 2, h * 128:(h + 1) * 128], vt[:, b, h * 256:(h + 1) * 256],
                        start=False, stop=True),
                         r=[('sTs', b % 2), ('vt', b)], w=[PK(bo)])
                state_block(b)
                for h in range(4):
                    bo = obanks[h // 2]
                    oc = slice((h % 2) * 256, (h % 2 + 1) * 256)
                    P.op('act', lambda e, h=h, bo=bo, oc=oc: e.activation(out=junk[:, :], in_=psb[bo][:, oc],
                                                                        func=AF.Square, accum_out=st_ss[:, h:h + 1]),
                         r=[PK(bo)], w=['junk', 'st_ss'])
                rstd_chain(4, 256, RMS_EPS)
                for h in range(4):
                    bo = obanks[h // 2]
                    oc = slice((h % 2) * 256, (h % 2 + 1) * 256)
                    P.op('dve', lambda e, h=h, bo=bo, oc=oc: e.scalar_tensor_tensor(
                        xn[:, b, h * 256:(h + 1) * 256], psb[bo][:, oc], st_r[:, h:h + 1],
                        rs[:, b, h * 256:(h + 1) * 256], ALU.mult, ALU.mult),
                         r=[PK(bo), 'st_r', ('rs', b)], w=[('xn', b)])
                unpin(obanks[0])
                unpin(obanks[1])

            gla_scores(0)
            kT_block(0)
            for b in range(nb):
                if b + 1 < nb:
                    gla_scores(b + 1)
                    kT_block(b + 1)
                gla_out(b)
                if ti == 0 and b == 0:
                    emit_cast_group(5)
            if ti == 0:
                emit_cast_group(6)
            bm = bank(pin=True)
            bq = bank(pin=True)
            def stats_mm(c):
                P.op('pe', lambda e: e.matmul(psb[bm][:, 0:TT], onesb[:, :], ybf[:, c % 2, 0:TT],
                                              start=(c == 0), stop=(c == 7)),
                     r=['onesb', ('ybf', c % 2)], w=[PK(bm)])
                P.op('pe', lambda e: e.matmul(psb[bq][:, 0:TT], onesb[:, :], ysq[:, c % 2, 0:TT],
                                              start=(c == 0), stop=(c == 7)),
                     r=['onesb', ('ysq', c % 2)], w=[PK(bq)])

            for c in range(8):
                bc = bank()
                for j in range(NPE_TAPS):
                    P.op('pe', lambda e, c=c, j=j, bc=bc: e.matmul(
                        psb[bc][:, 0:TT], diagR[:, c * NPE_TAPS + j, :], cbuf[:, c, j:j + TT],
                        start=(j == 0), stop=(j == NPE_TAPS - 1)),
                         r=['diagR', ('cbuf', c)], w=[PK(bc)])
                for j in range(NPE_TAPS, CK):
                    P.op('dve', lambda e, c=c, j=j, bc=bc: e.scalar_tensor_tensor(
                        psb[bc][:, 0:TT], cbuf[:, c, j:j + TT], cv(c, R_CW + j), psb[bc][:, 0:TT],
                        ALU.mult, ALU.add),
                         r=[('cbuf', c), 'colv', PK(bc)], w=[PK(bc)])
                yc, yk = y32_c(c)
                P.op('act', lambda e, c=c, bc=bc, yc=yc: e.activation(out=yc[:, 0:TT], in_=psb[bc][:, 0:TT],
                                                                    func=AF.Identity, bias=cv(c, R_CB)),
                     r=[PK(bc), 'colv'], w=[yk])
                P.op('act', lambda e, c=c, bc=bc: e.activation(out=ybf[:, c % 2, 0:TT], in_=psb[bc][:, 0:TT],
                                                             func=AF.Identity, bias=cv(c, R_CB)),
                     r=[PK(bc), 'colv'], w=[('ybf', c % 2)])
                P.op('act', lambda e, c=c, bc=bc: e.activation(out=ysq[:, c % 2, 0:TT], in_=psb[bc][:, 0:TT],
                                                             func=AF.Square, bias=cv(c, R_CB)),
                     r=[PK(bc), 'colv'], w=[('ysq', c % 2)])
                if c >= 1:
                    stats_mm(c - 1)
            stats_mm(7)
            P.op('pool', lambda e: e.tensor_copy(cbuf[:, :, 0:HALO], cbuf[:, :, TT:TT + HALO]),
                 r=[('cbuf', c) for c in range(8)], w=[('cbuf', c) for c in range(8)])
            if ti == 0:
                emit_cast_group(7)
            P.op('act', lambda e: e.activation(out=a_e[:, 0, 0:TT], in_=psb[bm][:, 0:TT], func=AF.Copy),
                 r=[PK(bm)], w=[('a_e', 0)])
            P.op('dve', lambda e: e.tensor_tensor(a_e[:, 1, 0:TT], psb[bm][:, 0:TT], a_e[:, 0, 0:TT], ALU.mult),
                 r=[PK(bm), ('a_e', 0)], w=[('a_e', 1)])
            P.op('dve', lambda e: e.tensor_tensor(a_e[:, 1, 0:TT], psb[bq][:, 0:TT], a_e[:, 1, 0:TT], ALU.subtract),
                 r=[PK(bq), ('a_e', 1)], w=[('a_e', 1)])
            P.op('act', lambda e: e.activation(out=a_e[:, 2, 0:TT], in_=a_e[:, 1, 0:TT], func=AF.Ln, bias=LN_EPS),
                 r=[('a_e', 1)], w=[('a_e', 2)])
            P.op('act', lambda e: e.activation(out=psb[bm][:, 0:TT], in_=a_e[:, 2, 0:TT], func=AF.Exp, scale=-0.5),
                 r=[('a_e', 2), ('a_e', 0)], w=[PK(bm)])
            P.op('dve', lambda e: e.scalar_tensor_tensor(psb[bq][:, 0:TT], a_e[:, 0, 0:TT], -1.0, psb[bm][:, 0:TT],
                                                         ALU.mult, ALU.mult),
                 r=[('a_e', 0), PK(bm), ('a_e', 1)], w=[PK(bq)])
            P.op('act', lambda e: e.activation(out=a_e[:, 1, 0:TT], in_=a_e[:, 2, 0:TT], func=AF.Exp, scale=-0.5),
                 r=[('a_e', 2)], w=[('a_e', 1)])
            P.op('pool', lambda e: e.tensor_tensor(a_e[:, 0, 0:TT], a_e[:, 0, 0:TT], a_e[:, 1, 0:TT], ALU.mult),
                 r=[('a_e', 0), ('a_e', 1)], w=[('a_e', 0)])
            ogT = Fm[1]
            transpose_block_set(xn, 'xn', nb, R_GG, ogT, 'F1')
            sc_ = Fm[0]
            dctr = 0
            for c in (0, 1, 3, 2, 4, 6, 5, 7):
                yc, yk = y32_c(c)
                if c in (2, 5, 7):
                    P.op('pool', lambda e, yc=yc: e.tensor_tensor(a_e[:, 2, 0:TT], yc[:, 0:TT], a_e[:, 1, 0:TT],
                                                                  ALU.mult),
                         r=[yk, ('a_e', 1)], w=[('a_e', 2)])
                    P.op('pool', lambda e: e.tensor_tensor(a_e[:, 2, 0:TT], a_e[:, 2, 0:TT], a_e[:, 0, 0:TT],
                                                           ALU.subtract),
                         r=[('a_e', 2), ('a_e', 0)], w=[('a_e', 2)])
                    P.op('act', lambda e, c=c: e.activation(out=sc_[:, c, 0:TT], in_=a_e[:, 2, 0:TT],
                                                           func=AF.Silu, scale=cv(c, R_LG), bias=cv(c, R_LB)),
                         r=[('a_e', 2), 'colv'], w=[('F0', c)])
                    continue
                zs = dctr % 2
                dctr += 1
                P.op('dve', lambda e, c=c, zs=zs, yc=yc: e.tensor_tensor(mtmp[:, zs, 0:TT], yc[:, 0:TT],
                                                                        psb[bm][:, 0:TT], ALU.mult),
                     r=[yk, PK(bm)], w=[('mtmp', zs)])
                P.op('dve', lambda e, c=c, zs=zs: e.tensor_tensor(mtmp[:, zs, 0:TT], mtmp[:, zs, 0:TT],
                                                                 psb[bq][:, 0:TT], ALU.add),
                     r=[('mtmp', zs), PK(bq)], w=[('mtmp', zs)])
                P.op('act', lambda e, c=c, zs=zs: e.activation(out=sc_[:, c, 0:TT], in_=mtmp[:, zs, 0:TT],
                                                             func=AF.Silu, scale=cv(c, R_LG), bias=cv(c, R_LB)),
                     r=[('mtmp', zs), 'colv'], w=[('F0', c)])
            unpin(bm)
            unpin(bq)
            for i in range(4):
                wp, wkey = piece(('go', i))
                for jc in range(2):
                    m = 2 * i + jc
                    b_ = bank()
                    ws_matmul(wp, wkey, jc, ogT, 'F1', TT, b_)
                    P.op('dve', lambda e, m=m, b_=b_: e.tensor_tensor(sg1[:, m, 0:TT], psb[b_][:, 0:TT],
                                                                    sg1[:, m, 0:TT], ALU.mult),
                         r=[PK(b_), ('sg1', m)], w=[('sg1', m)])
            mg = Fm[1]
            for i in range(4):
                wp, wkey = piece(('co', i))
                for jc in range(2):
                    m = 2 * i + jc
                    b_ = bank()
                    ws_matmul(wp, wkey, jc, sc_, 'F0', TT, b_)
                    P.op('dve', lambda e, m=m, b_=b_: e.tensor_tensor(mtmp[:, m % 2, 0:TT], psb[b_][:, 0:TT],
                                                                    sg2[:, m, 0:TT], ALU.mult),
                         r=[PK(b_), ('sg2', m)], w=[('mtmp', m % 2)])
                    P.op('pool', lambda e, m=m: e.tensor_tensor(mg[:, m, 0:TT], mtmp[:, m % 2, 0:TT],
                                                               sg1[:, m, 0:TT], ALU.add),
                         r=[('mtmp', m % 2), ('sg1', m)], w=[('F1', m)])
            if ti == 0:
                emit_cast_group(8)
            for g in range(2):
                def evac(b, bk, g=g):
                    P.op('dve', lambda e: e.tensor_tensor(H[:, b, g * 512:(g + 1) * 512], psb[bk][:, 0:512],
                                                          H[:, b, g * 512:(g + 1) * 512], ALU.add),
                         r=[PK(bk), HK(b)], w=[HK(b)])
                as_group([('wo', g, 0), ('wo', g, 1)], fm_c(mg, 'F1'), nb, [4, 4], evac)
            u2 = Fm[0]
            norm_transpose(H, slot, nb, R_G2, u2, 'F0')
            if ti + 1 < nfull_tiles:
                next_slot[ti + 1] = tile_head_norm(ti + 1)
            P.op('pool', lambda e: e.tensor_copy(Fm0[:, :, 0:2], u2h[:, :, :]), r=['u2h'], w=['F0h'])
            P.op('pool', lambda e: e.tensor_copy(u2h[:, :, :], Fm0[:, :, TT:TT + 2]),
                 r=[('F0', c) for c in range(8)] + ['F0h'], w=['u2h'])
            for i in range(11):
                wpa, wka = piece(('ua', i))
                abanks = []
                for jc in range(2):
                    ba = bank(pin=True)
                    abanks.append(ba)
                    for k in range(8):
                        P.op('pe', lambda e, k=k, jc=jc, ba=ba, wpa=wpa: e.matmul(
                            psb[ba][:, 0:TT + 2], wpa[:, k, jc * 128:(jc + 1) * 128], Fm0[:, k, 0:TT + 2],
                            start=(k == 0), stop=(k == 7)),
                             r=[wka, ('F0', k), 'F0h'], w=[PK(ba)])
                wpb, wkb = piece(('ub', i))
                for jc in range(2):
                    m = 2 * i + jc
                    ba = abanks[jc]
                    bb = bank()
                    ws_matmul(wpb, wkb, jc, u2, 'F0', TT, bb)
                    fa = m % 3
                    A_ = a_e[:, fa, :]
                    AK = ('a_e', fa)
                    P.op('act', lambda e, m=m, ba=ba, A_=A_: e.activation(
                        out=A_[:, 0:TT], in_=psb[ba][:, 2:TT + 2], func=AF.Identity, scale=ffv(m, 2),
                        bias=ffv(m, 3)), r=[PK(ba), 'colv'], w=[AK])
                    P.op('dve', lambda e, m=m, ba=ba, A_=A_: e.scalar_tensor_tensor(
                        A_[:, 0:TT], psb[ba][:, 1:TT + 1], ffv(m, 1), A_[:, 0:TT], ALU.mult, ALU.add),
                         r=[PK(ba), 'colv', AK], w=[AK])
                    P.op('dve', lambda e, m=m, ba=ba, A_=A_: e.scalar_tensor_tensor(
                        A_[:, 0:TT], psb[ba][:, 0:TT], ffv(m, 0), A_[:, 0:TT], ALU.mult, ALU.add),
                         r=[PK(ba), 'colv', AK], w=[AK])
                    unpin(ba)
                    P.op('act', lambda e, m=m, A_=A_: e.activation(out=a_B[:, m % 2, 0:TT], in_=A_[:, 0:TT],
                                                                 func=AF.Silu), r=[AK], w=[('a_B', m % 2)])
                    ac_, ak_ = actT_c(m)
                    P.op('dve', lambda e, m=m, bb=bb, ac_=ac_: e.tensor_tensor(ac_[:, 0:TT], psb[bb][:, 0:TT],
                                                                             a_B[:, m % 2, 0:TT], ALU.mult),
                         r=[PK(bb), ('a_B', m % 2)], w=[ak_])
            if ti + 1 < nfull_tiles:
                tile_head_tr()
            for g in range(2):
                def evac(b, bk, g=g):
                    P.op('dve', lambda e: e.tensor_tensor(H[:, b, g * 512:(g + 1) * 512], psb[bk][:, 0:512],
                                                          H[:, b, g * 512:(g + 1) * 512], ALU.add),
                         r=[PK(bk), HK(b)], w=[HK(b)])
                as_group([('dn', g, kp) for kp in range(6)], lambda kc: actT_c(kc), nb, [4, 4, 4, 4, 4, 2], evac)
            if ti + 1 < nfull_tiles:
                alpha_and_k(Fm[0], TT, nb, True)
            for b in range(nb):
                P.op('act', lambda e, b=b: e.activation(out=rs[:, b, :], in_=H[:, b, :], func=AF.Square,
                                                        accum_out=st_ss[:, b:b + 1]),
                     r=[HK(b)], w=[('rs', b), 'st_ss'])
            rstd_chain(nb, D, RMS_EPS)
            for b in range(nb):
                P.op('dve', lambda e, b=b: e.scalar_tensor_tensor(H[:, b, :], H[:, b, :], st_r[:, b:b + 1],
                                                                  fgb[:, :], ALU.mult, ALU.mult),
                     r=[HK(b), 'st_r', 'fgb'], w=[HK(b)])
            def do_store(ti=ti, slot=slot, H=H, nb=nb, TT=TT):
                P.dma('sp', y[ti * TT:(ti + 1) * TT, :].rearrange("(b p) d -> p b d", p=128), H[:, 0:nb, :],
                      key=('hst', slot), r=[('h', slot, b) for b in range(nb)], is_out=True)
            pending_store.append(do_store)
            if ti + 1 == nfull_tiles:
                while pending_store:
                    pending_store.pop(0)()

        for ti in range(nfull_tiles):
            full_tile(ti)
        assert stream['pos'] == len(order), (stream['pos'], len(order))
        P.emit()
    return nc


NPRE_BLOCKS = 32
NFULL_BLOCKS = 33
_CACHE = {}


def make_core_inputs(inputs, npre, nfull, cores=None):
    x = np.asarray(inputs["x"], dtype=np.float32)
    B, S, _ = x.shape
    meta = np.asarray(inputs["meta_tokens"], dtype=np.float32)
    sq = lambda k: np.ascontiguousarray(np.asarray(inputs[k], dtype=np.float32)[0])
    common = {
        "w_in": sq("w_in"), "w_alpha_up": sq("w_alpha_up"), "w_gla_o": sq("w_gla_o"),
        "w_conf_o": sq("w_conf_o"), "w_out": sq("w_out"), "w_up": sq("w_up"), "w_down": sq("w_down"),
        "conf_dw_w": sq("conf_dw_w"), "ffn_dw_w": sq("ffn_dw_w"),
        "identf": np.eye(128, dtype=np.float32),
        "cmask": np.triu(np.ones((128, 128), dtype=np.float32)),
    }
    for nm in ("norm_mix_g", "gla_norm_g", "conf_dw_b", "conf_ln_g", "conf_ln_b", "norm_ffn_g",
               "b_alpha", "ffn_dw_b"):
        common[nm] = sq(nm).reshape(1, -1)
    common["final_norm_g"] = np.asarray(inputs["final_norm_g"], dtype=np.float32).reshape(1, -1)
    in_maps = []
    for b in range(B):
        seq = np.concatenate([np.zeros((112, D), np.float32), meta, x[b]], axis=0)
        nblk = seq.shape[0] // 128
        ma = dict(common)
        ma["xp"] = np.zeros((max(npre, 1) * 128, D), np.float32)
        ma["xf"] = np.ascontiguousarray(seq[0:nfull * 128])
        in_maps.append(ma)
        mb = dict(common)
        mb["xp"] = np.ascontiguousarray(seq[0:max(npre, 1) * 128])
        mb["xf"] = np.ascontiguousarray(seq[(nblk - nfull) * 128:nblk * 128])
        in_maps.append(mb)
    return in_maps


def kernel(**inputs):
    x = np.asarray(inputs["x"])
    B, S, _ = x.shape
    if "nc" not in _CACHE:
        _CACHE["nc"] = build_program(NPRE_BLOCKS, NFULL_BLOCKS)
    nc = _CACHE["nc"]
    in_maps = make_core_inputs(inputs, NPRE_BLOCKS, NFULL_BLOCKS)
    res = run_bass_kernel_spmd(nc, in_maps, core_ids=list(range(2 * B)))
    out = np.empty((B, S, D), np.float32)
    half = (NFULL_BLOCKS - 1) * 128
    for b in range(B):
        ya = np.asarray(res.results[2 * b]["y"])
        yb = np.asarray(res.results[2 * b + 1]["y"])
        out[b, 0:half] = ya[128:128 + half]
        out[b, S - half:S] = yb[128:128 + half]
    return out
```

```python
import numpy as np
from contextlib import ExitStack
import concourse.bass as bass
import concourse.mybir as mybir
from concourse.bass_utils import run_bass_kernel_spmd

F32 = mybir.dt.float32
BF16 = mybir.dt.bfloat16
AF = mybir.ActivationFunctionType
ALU = mybir.AluOpType

D = 1024
NB = 3
TTM = NB * 128
DFF = 2816
NFC = DFF // 128
CK = 31
HALO = CK - 1
N_IN = 7184
RMS_EPS = 1e-6
LN_EPS = 1e-5
NSLOT = 5
import os as _os
NPE_TAPS = int(_os.environ.get('K_TAPS', '29'))
K_PRE = int(_os.environ.get('K_PRE', '199'))
K_CASTINT = int(_os.environ.get('K_CASTINT', '1'))
K_PEATT = int(_os.environ.get('K_PEATT', '0'))
NPSUM = 8

R_FG, R_G1, R_GG, R_CB, R_LG, R_LB, R_G2, R_CW, R_BA, R_FF = 0, 1, 2, 3, 4, 5, 6, 7, 38, 39
NROW = 64


class Prog:
    CE = ('pe', 'act', 'dve', 'pool')
    K = 8
    FINAL = 1 << 40

    def __init__(self, nc, stack):
        self.nc = nc
        self.stack = stack
        self.ops = {e: [] for e in ('pe', 'act', 'dve', 'pool', 'sp')}
        self.sems = {e: [stack.enter_context(nc.semaphore(f"s_{e}{i}")) for i in range(self.K)]
                     for e in self.CE}
        self.cnt = {e: 0 for e in self.CE}
        self.seen = {e: {} for e in self.ops}
        self.res = {}
        self.dsem = {}
        self.out_events = []

    def _wait_for(self, eng, ev, waits):
        kind, src, val = ev
        key = (kind, src)
        if self.seen[eng].get(key, -1) >= val:
            return
        self.seen[eng][key] = val
        if kind == 'eng':
            waits.append((self.sems[src][val % self.K], val // self.K + 1))
        elif val == self.FINAL:
            waits.append((self.dsem[src], None))
        else:
            waits.append((self.dsem[src][0], val))

    def _deps(self, eng, r, w):
        waits = []
        evs = []
        for k in r:
            st = self.res.get(k)
            if st and st[0] is not None:
                evs.append((st[0], True))
        for k in w:
            st = self.res.get(k)
            if st:
                if st[0] is not None:
                    evs.append((st[0], False))
                for ev in st[1].values():
                    evs.append((ev, False))
        for ev, raw in evs:
            if ev[0] == 'eng' and ev[1] == eng and eng == 'pe':
                continue
            self._wait_for(eng, ev, waits)
        return waits

    def _commit(self, ev, r, w):
        for k in r:
            st = self.res.setdefault(k, [None, {}])
            st[1][(ev[0], ev[1])] = ev
        for k in w:
            self.res[k] = [ev, {}]

    def op(self, eng, emit, r=(), w=()):
        waits = self._deps(eng, r, w)
        n = self.cnt[eng]
        self.cnt[eng] = n + 1
        ev = ('eng', eng, n)
        self.ops[eng].append((emit, waits, (self.sems[eng][n % self.K], 1), False))
        self._commit(ev, r, w)

    def dma(self, eng, out, in_, key, r=(), w=(), is_out=False, group=False):
        if key not in self.dsem:
            self.dsem[key] = [self.stack.enter_context(self.nc.semaphore(f"d_{len(self.dsem)}")), 0]
        waits = self._deps(eng, r, w)
        self.dsem[key][1] += 16
        val = self.FINAL if group else self.dsem[key][1]
        ev = ('dma', key, val)
        self.ops[eng].append((lambda e: e.dma_start(out=out, in_=in_), waits, (self.dsem[key][0], 16), True))
        self._commit(ev, r, w)
        if is_out:
            self.out_events.append(ev)

    def emit(self):
        nc = self.nc
        fin = []
        for ev in self.out_events:
            self._wait_for('sp', ev, fin)
        with nc.Block() as block:
            def run(e, name, final=None):
                attach = name in (('pe', 'act', 'dve', 'pool') if K_PEATT else ('act', 'dve', 'pool'))
                for emit, waits, inc, isdma in self.ops[name]:
                    ws = [(s_[0], s_[1]) if v is None else (s_, v) for s_, v in waits]
                    first = ws.pop(0) if (attach and ws and not isdma) else None
                    for s, v in ws:
                        e.wait_ge(s, v)
                    ins = emit(e)
                    if first is not None:
                        ins._wait_ge(first[0], first[1])
                    ins.then_inc(inc[0], inc[1])
                if final:
                    for s_, v in final:
                        if v is None:
                            s_, v = s_[0], s_[1]
                        e.wait_ge(s_, v)

            @block.tensor
            def _(e):
                run(e, 'pe')

            @block.scalar
            def _(e):
                run(e, 'act')

            @block.vector
            def _(e):
                run(e, 'dve')

            @block.gpsimd
            def _(e):
                run(e, 'pool')

            @block.sync
            def _(e):
                run(e, 'sp', fin)


def build_program(npre_blocks, nfull_blocks):
    nc = bass.Bass("TRN2", target_bir_lowering=False)
    dt = lambda name, shape, dtype, kind: nc.dram_tensor(name, shape, dtype, kind=kind).ap()
    xp = dt("xp", [max(npre_blocks, 1) * 128, D], F32, "ExternalInput")
    xf = dt("xf", [nfull_blocks * 128, D], F32, "ExternalInput")
    w_in = dt("w_in", [D, N_IN], F32, "ExternalInput")
    w_au = dt("w_alpha_up", [16, 512], F32, "ExternalInput")
    w_go = dt("w_gla_o", [D, D], F32, "ExternalInput")
    w_co = dt("w_conf_o", [D, D], F32, "ExternalInput")
    w_o = dt("w_out", [D, D], F32, "ExternalInput")
    w_up = dt("w_up", [D, 2 * DFF], F32, "ExternalInput")
    w_dn = dt("w_down", [DFF, D], F32, "ExternalInput")
    vin = {}
    for nm, n in (("final_norm_g", D), ("norm_mix_g", D), ("gla_norm_g", D), ("conf_dw_b", D),
                  ("conf_ln_g", D), ("conf_ln_b", D), ("norm_ffn_g", D), ("b_alpha", 512),
                  ("ffn_dw_b", DFF)):
        vin[nm] = dt(nm, [1, n], F32, "ExternalInput")
    cdw = dt("conf_dw_w", [CK, D], F32, "ExternalInput")
    fdw = dt("ffn_dw_w", [3, DFF], F32, "ExternalInput")
    identf_d = dt("identf", [128, 128], F32, "ExternalInput")
    mask_d = dt("cmask", [128, 128], F32, "ExternalInput")
    y = dt("y", [nfull_blocks * 128, D], F32, "ExternalOutput")

    pieces = []

    def ws_piece(W, c0, ncols=256):
        return (W[:, c0:c0 + ncols].rearrange("(k p) j -> p k j", p=128), 8, ncols)

    def as_piece(W, k0, nk, c0):
        return (W[k0 * 128:(k0 + nk) * 128, c0:c0 + 512].rearrange("(k p) j -> p k j", p=128), nk, 512)

    pid = {}

    def add(name, spec):
        pid[name] = len(pieces)
        pieces.append(spec)

    add('alr', ws_piece(w_in, 3072, 16))
    for i in range(2):
        add(('k', i), ws_piece(w_in, 512 + 256 * i))
    for g in range(2):
        for kh in range(2):
            add(('v', g, kh), as_piece(w_in, 4 * kh, 4, 1024 + 512 * g))
    for i in range(2):
        add(('q', i), ws_piece(w_in, 256 * i))
    for g in range(2):
        for kh in range(2):
            add(('r', g, kh), as_piece(w_in, 4 * kh, 4, 2048 + 512 * g))
    for i in range(4):
        add(('c2', i), ws_piece(w_in, 4112 + 256 * i))
        add(('c1', i), ws_piece(w_in, 3088 + 256 * i))
    for i in range(4):
        add(('gg', i), ws_piece(w_in, 5136 + 256 * i))
    for i in range(4):
        add(('gc', i), ws_piece(w_in, 6160 + 256 * i))
    for i in range(4):
        add(('go', i), ws_piece(w_go, 256 * i))
    for i in range(4):
        add(('co', i), ws_piece(w_co, 256 * i))
    for g in range(2):
        for kh in range(2):
            add(('wo', g, kh), as_piece(w_o, 4 * kh, 4, 512 * g))
    for i in range(11):
        add(('ua', i), ws_piece(w_up, 256 * i))
        add(('ub', i), ws_piece(w_up, DFF + 256 * i))
    for g in range(2):
        for kp in range(6):
            nk = 4 if kp < 5 else 2
            add(('dn', g, kp), as_piece(w_dn, 4 * kp, nk, 512 * g))
    NP = len(pieces)
    wsc = nc.dram_tensor("wsc", [NP, 128, 2048], BF16, kind="Internal").ap()

    with ExitStack() as stack:
        P = Prog(nc, stack)
        sb = lambda name, shape, dtype: stack.enter_context(nc.sbuf_tensor(name, shape, dtype))
        hb = [sb(f"hb{i}", [128, NB, D], F32) for i in range(2)]
        xn = sb("xn", [128, NB, D], BF16)
        Fm0 = sb("Fm0", [128, 8, 2 + TTM], BF16)
        Fm = [Fm0[:, :, 2:2 + TTM], sb("Fm1", [128, 8, TTM], BF16)]
        u2h = sb("u2h", [128, 8, 2], BF16)
        qt = sb("qt", [128, 4, TTM], BF16)
        kt = sb("kt", [128, 4, TTM], BF16)
        alr_sb = sb("alr_sb", [16, TTM], BF16)
        a_e = sb("a_e", [128, 3, TTM], F32)
        a_B = sb("a_B", [128, 2, TTM], F32)
        ebt = sb("ebt", [128, 4, TTM], F32)
        enbt = sb("enbt", [128, 4, TTM], F32)
        vt = sb("vt", [128, NB, D], BF16)
        rs = sb("rs", [128, NB, D], BF16)
        cbuf = sb("cbuf", [128, 8, HALO + TTM], BF16)
        sg1 = sb("sg1", [128, 8, TTM], BF16)
        sg2 = sb("sg2", [128, 8, TTM], BF16)
        S32 = sb("S32", [128, 4, 256], F32)
        Sdec = sb("Sdec", [128, 2, 256], F32)
        Sbf = sb("Sbf", [128, 4, 256], BF16)
        kTs = sb("kTs", [128, 2, 512], BF16)
        sTs = sb("sTs", [128, 2, 512], BF16)
        junk = sb("junk", [128, 256], BF16)
        mtmp = sb("mtmp", [128, 2, TTM], F32)
        ybf = sb("ybf", [128, 2, TTM], BF16)
        ysq = sb("ysq", [128, 2, TTM], BF16)
        diagR = sb("diagR", [128, 8 * NPE_TAPS, 128], BF16)
        colv = sb("colv", [128, 8, NROW], F32)
        negb = sb("negb", [128, 4], F32)
        fgb = sb("fgb", [128, D], F32)
        identb = sb("identb", [128, 128], BF16)
        identf = sb("identf_s", [128, 128], F32)
        maskt = sb("maskt", [128, 128], F32)
        mask4 = sb("mask4", [128, 512], BF16)
        onesb = sb("onesb", [128, 128], BF16)
        wau = sb("wau", [16, 512], BF16)
        rmask = sb("rmask", [128, TTM], F32)
        st_ss = sb("st_ss", [128, 16], F32)
        st_l = sb("st_l", [128, 16], F32)
        st_r = sb("st_r", [128, 16], F32)
        wsl = [sb(f"wsl{i}", [128, 2048], BF16) for i in range(NSLOT)]
        psb = [stack.enter_context(nc.psum_tensor(f"ps{i}", [128, 512], F32)) for i in range(NPSUM)]

        def actT_c(m):
            if m < 8:
                return sg1[:, m, :], ('sg1', m)
            if m < 16:
                return sg2[:, m - 8, :], ('sg2', m - 8)
            if m < 20:
                return qt[:, m - 16, :], ('qt', m - 16)
            return kt[:, m - 20, :], ('kt', m - 20)

        def y32_c(c):
            return (ebt[:, c, :], ('eb', c)) if c < 4 else (enbt[:, c - 4, :], ('enb', c - 4))

        def fm_c(buf, key):
            return lambda kc: (buf[:, kc, :], (key, kc))

        bank_state = {'next': 0, 'pinned': set()}

        def bank(pin=False):
            for _ in range(NPSUM):
                b = bank_state['next']
                bank_state['next'] = (b + 1) % NPSUM
                if b not in bank_state['pinned']:
                    if pin:
                        bank_state['pinned'].add(b)
                    return b
            raise RuntimeError("all PSUM banks pinned")

        def unpin(b):
            bank_state['pinned'].discard(b)

        PK = lambda b: ('ps', b)

        stream = {'pos': 0, 'issued': 0, 'order': []}

        def issue_to(n):
            while stream['issued'] < min(n, len(stream['order'])):
                i = stream['issued']
                p = stream['order'][i]
                s = i % NSLOT
                n_ = pieces[p][1] * pieces[p][2]
                assert ('wsc', p) in P.res, "stream DMA recorded before its cast DMA"
                P.dma('sp', wsl[s][:, 0:n_], wsc[p][:, 0:n_], key=('wsl', s), r=[('wsc', p)], w=[('wsl', s)])
                stream['issued'] += 1

        def piece(name):
            i = stream['pos']
            assert stream['order'][i] == pid[name], (name, i)
            issue_to(i + NSLOT)
            stream['pos'] += 1
            s = i % NSLOT
            _, k, j = pieces[pid[name]]
            return wsl[s][:, 0:k * j].rearrange("p (k j) -> p k j", k=k), ('wsl', s)

        P.dma('pool', wau[:, :], w_au[:, :], key='cst_wau', w=['wau'])
        cast_groups = {}
        for p_i, (src, k, j) in enumerate(pieces):
            dst = wsc[p_i][:, 0:k * j].rearrange("p (k j) -> p k j", k=k)
            gi = 0 if p_i < 7 else 1 + (p_i - 7) // 9
            cast_groups.setdefault(gi, []).append((p_i, dst, src))

        def emit_cast_group(gi):
            for p_i, dst, src in cast_groups.pop(gi, []):
                P.dma('pool', dst, src, key=('cast', gi), w=[('wsc', p_i)], group=True)

        emit_cast_group(0)
        if not K_CASTINT:
            for gi in sorted(cast_groups):
                emit_cast_group(gi)
        P.dma('sp', identf[:, :], identf_d[:, :], key='cst', w=['identf'], group=True)
        P.dma('sp', maskt[:, :], mask_d[:, :], key='cst', w=['mask'], group=True)
        P.dma('sp', fgb[:, :], vin["final_norm_g"].partition_broadcast(128), key='cst', w=['fgb'], group=True)
        vecs = hb[1]
        VK = [('h', 1, b) for b in range(NB)]
        vrows = []
        for r_, nm in [(R_G1, "norm_mix_g"), (R_GG, "gla_norm_g"), (R_CB, "conf_dw_b"), (R_LG, "conf_ln_g"),
                       (R_LB, "conf_ln_b"), (R_G2, "norm_ffn_g")]:
            vrows.append((r_, 1, D, vin[nm][:, :]))
        vrows.append((R_CW, CK, D, cdw[:, :]))
        vrows.append((R_BA, 1, 512, vin["b_alpha"][:, :]))
        for v_ in range(4):
            src = fdw[v_:v_ + 1, :] if v_ < 3 else vin["ffn_dw_b"][:, :]
            for s_ in range(3):
                n_ = 1024 if s_ < 2 else DFF - 2048
                vrows.append((R_FF + v_ * 3 + s_, 1, n_, src[:, s_ * 1024:s_ * 1024 + n_]))
        VRK = [('vecs', i) for i in range(len(vrows))]
        P.op('dve', lambda e: e.memset(vecs[0:NROW, 0, :], 0.0), w=VK + VRK)
        for i, (r_, nr, n_, src) in enumerate(vrows):
            P.dma('sp', vecs[r_:r_ + nr, 0, 0:n_], src, key='cst', w=[('vecs', i)], group=True)
        VK = VK + VRK
        P.op('act', lambda e: e.activation(out=identb[:, :], in_=identf[:, :], func=AF.Copy),
             r=['identf'], w=['identb'])
        P.op('dve', lambda e: e.memset(onesb[:, :], 1.0 / D), w=['onesb'])
        for h in range(4):
            P.op('dve', lambda e, h=h: e.tensor_copy(mask4[:, h * 128:(h + 1) * 128], maskt[:, :]), r=['mask'], w=['mask4'])
        P.op('dve', lambda e: e.memset(rmask[:, :], 1.0), w=['rmask'])
        for b in range(NB):
            P.op('dve', lambda e, b=b: e.memset(rmask[:, b * 128:b * 128 + 1], 0.0), w=['rmask'])
        P.op('dve', lambda e: e.memset(cbuf[:, :, 0:HALO], 0.0), w=[('cbuf', c) for c in range(8)])
        P.op('dve', lambda e: e.memset(u2h[:, :, :], 0.0), w=['u2h'])
        P.op('dve', lambda e: e.memset(S32[:, :, :], 0.0), w=[('S32', h) for h in range(4)])
        P.op('dve', lambda e: e.memset(Sbf[:, :, :], 0.0), w=[('Sbf', h) for h in range(4)])
        for c in range(8):
            b_ = bank()
            P.op('pe', lambda e, b_=b_, c=c: e.transpose(psb[b_][:, 0:NROW], vecs[0:NROW, 0, c * 128:(c + 1) * 128],
                                                        identf[0:NROW, 0:NROW]),
                 r=VK + ['identf'], w=[PK(b_)])
            P.op('dve', lambda e, b_=b_, c=c: e.tensor_copy(colv[:, c, :], psb[b_][:, 0:NROW]),
                 r=[PK(b_)], w=['colv'])
        P.op('dve', lambda e: e.tensor_scalar(negb[:, :], colv[:, 0:4, R_BA], -1.0, None, ALU.mult),
             r=['colv'], w=['negb'])

        def cv(c, r_):
            return colv[:, c, r_:r_ + 1]

        diag_todo = [(c, j) for c in range(8) for j in range(NPE_TAPS)]

        def emit_diag(n):
            for _ in range(min(n, len(diag_todo))):
                c, j = diag_todo.pop(0)
                P.op('pool', lambda e, c=c, j=j: e.tensor_scalar(
                    diagR[:, c * NPE_TAPS + j, :], identb[:, :], cv(c, R_CW + j), 0.0, ALU.mult, ALU.add),
                     r=['identb', 'colv'], w=['diagR'])

        def ffv(m, v_):
            s_, c = divmod(m, 8)
            return colv[:, c, R_FF + v_ * 3 + s_:R_FF + v_ * 3 + s_ + 1]

        tilectr = {'n': 0}

        def rstd_chain(ncols, dim, eps):
            P.op('act', lambda e: e.activation(out=st_l[:, 0:ncols], in_=st_ss[:, 0:ncols], func=AF.Ln,
                                               scale=1.0 / dim, bias=eps), r=['st_ss'], w=['st_l'])
            P.op('act', lambda e: e.activation(out=st_r[:, 0:ncols], in_=st_l[:, 0:ncols], func=AF.Exp,
                                               scale=-0.5), r=['st_l'], w=['st_r'])

        def norm_transpose(hbuf, slot, nb, grow, dst, dkey, pre=False):
            for b in range(nb):
                P.op('act', lambda e, b=b: e.activation(out=xn[:, b, :], in_=hbuf[:, b, :], func=AF.Square,
                                                        accum_out=st_ss[:, b:b + 1]),
                     r=[('h', slot, b)], w=[('xn', b), 'st_ss'])
            rstd_chain(nb, D, RMS_EPS)
            for b in range(nb):
                if (pre and (K_PRE & 1)) or (not pre and (K_PRE & 64)):
                    P.op('dve', lambda e, b=b: e.tensor_scalar(xn[:, b, :], hbuf[:, b, :], st_r[:, b:b + 1], None,
                                                               ALU.mult),
                         r=[('h', slot, b), 'st_r'], w=[('xn', b)])
                else:
                    P.op('act', lambda e, b=b: e.activation(out=xn[:, b, :], in_=hbuf[:, b, :], func=AF.Copy,
                                                            scale=st_r[:, b:b + 1]),
                         r=[('h', slot, b), 'st_r'], w=[('xn', b)])
            if dst is not None:
                transpose_block_set(xn, 'xn', nb, grow, dst, dkey, pre)

        def transpose_block_set(src, skey, nb, grow, dst, dkey, pre=False):
            for b in range(nb):
                b_ = bank()
                pst = psb[b_][:, :].bitcast(BF16)
                for c in range(8):
                    P.op('pe', lambda e, b=b, c=c, pst=pst: e.transpose(pst[:, c * 128:(c + 1) * 128],
                                                                       src[:, b, c * 128:(c + 1) * 128],
                                                                       identb[:, :]),
                         r=[(skey, b), 'identb'], w=[PK(b_)])
                for c in range(8):
                    if pre and (K_PRE & 16) and c % 2 == 1:
                        P.op('dve', lambda e, b=b, c=c, pst=pst: e.tensor_scalar(
                            dst[:, c, b * 128:(b + 1) * 128], pst[:, c * 128:(c + 1) * 128], cv(c, grow), None,
                            ALU.mult),
                             r=[PK(b_), 'colv'], w=[(dkey, c)])
                    else:
                        P.op('act', lambda e, b=b, c=c, pst=pst: e.activation(
                            out=dst[:, c, b * 128:(b + 1) * 128], in_=pst[:, c * 128:(c + 1) * 128],
                            func=AF.Copy, scale=cv(c, grow)),
                             r=[PK(b_), 'colv'], w=[(dkey, c)])

        def ws_matmul(wp, wkey, jc, src, skey, TT, b_):
            for k in range(8):
                P.op('pe', lambda e, k=k: e.matmul(psb[b_][:, 0:TT], wp[:, k, jc * 128:(jc + 1) * 128],
                                                  src[:, k, 0:TT], start=(k == 0), stop=(k == 7)),
                     r=[wkey, (skey, k)], w=[PK(b_)])

        def as_group(names, srcf, nb, nkc_list, evac):
            banks = [bank(pin=True) for _ in range(nb)]
            kbase = 0
            npieces = len(names)
            for pi, nm in enumerate(names):
                wp, wkey = piece(nm)
                nk = nkc_list[pi]
                for b in range(nb):
                    for kk in range(nk):
                        kc = kbase + kk
                        st_ = (kc == 0)
                        sp_ = (pi == npieces - 1 and kk == nk - 1)
                        sap, skey_ = srcf(kc)
                        P.op('pe', lambda e, b=b, kk=kk, sap=sap, wp=wp, st_=st_, sp_=sp_, bk=banks[b]: e.matmul(
                            psb[bk][:, 0:512], sap[:, b * 128:(b + 1) * 128], wp[:, kk, :],
                            start=st_, stop=sp_),
                             r=[wkey, skey_], w=[PK(banks[b])])
                kbase += nk
            for b in range(nb):
                evac(b, banks[b])
                unpin(banks[b])

        def alpha_and_k(uT, TT, nb, full, fk='F0'):
            wp, wkey = piece('alr')
            b_ = bank()
            for k in range(8):
                P.op('pe', lambda e, k=k, b_=b_, wp=wp: e.matmul(psb[b_][0:16, 0:TT], wp[:, k, 0:16], uT[:, k, 0:TT],
                                                              start=(k == 0), stop=(k == 7)),
                     r=[wkey, (fk, k)], w=[PK(b_)])
            P.op('act', lambda e, b_=b_: e.activation(out=alr_sb[:, 0:TT], in_=psb[b_][0:16, 0:TT], func=AF.Copy),
                 r=[PK(b_)], w=['alr'])
            for h in range(4):
                bz = bank()
                P.op('pe', lambda e, h=h, bz=bz: e.matmul(psb[bz][:, 0:TT], wau[:, h * 128:(h + 1) * 128],
                                                        alr_sb[:, 0:TT], start=True, stop=True),
                     r=['wau', 'alr'], w=[PK(bz)])
                P.op('act', lambda e, h=h, bz=bz: e.activation(out=enbt[:, h, 0:TT], in_=psb[bz][:, 0:TT],
                                                             func=AF.Exp, scale=-1.0, bias=negb[:, h:h + 1]),
                     r=[PK(bz), 'negb'], w=[('enb', h)])
            for h in range(4):
                P.op('act', lambda e, h=h: e.activation(out=enbt[:, h, 0:TT], in_=enbt[:, h, 0:TT], func=AF.Ln,
                                                       bias=1.0), r=[('enb', h)], w=[('enb', h)])
            for h in range(4):
                P.op('dve', lambda e, h=h: e.tensor_tensor_scan(a_B[:, h % 2, 0:TT], rmask[:, 0:TT],
                                                               enbt[:, h, 0:TT], 0.0, ALU.mult, ALU.add),
                     r=[('enb', h), 'rmask'], w=[('a_B', h % 2)])
                if full:
                    P.op('act', lambda e, h=h: e.activation(out=ebt[:, h, 0:TT], in_=a_B[:, h % 2, 0:TT],
                                                           func=AF.Exp, scale=-1.0 / 16), r=[('a_B', h % 2)],
                         w=[('eb', h)])
                else:
                    for b in range(nb):
                        c_ = b * 128 + 127
                        P.op('act', lambda e, h=h, c_=c_: e.activation(out=ebt[:, h, c_:c_ + 1],
                                                                     in_=a_B[:, h % 2, c_:c_ + 1],
                                                                     func=AF.Exp, scale=-1.0 / 16),
                             r=[('a_B', h % 2)], w=[('eb', h)])
                P.op('act', lambda e, h=h: e.activation(out=enbt[:, h, 0:TT], in_=a_B[:, h % 2, 0:TT],
                                                       func=AF.Exp, scale=1.0 / 16), r=[('a_B', h % 2)],
                     w=[('enb', h)])

        def k_proj(uT, TT, fk='F0'):
            for i in range(2):
                wp, wkey = piece(('k', i))
                for jc in range(2):
                    h = 2 * i + jc
                    b_ = bank()
                    ws_matmul(wp, wkey, jc, uT, fk, TT, b_)
                    P.op('dve', lambda e, h=h, b_=b_: e.tensor_tensor(kt[:, h, 0:TT], psb[b_][:, 0:TT],
                                                                    enbt[:, h, 0:TT], ALU.mult),
                         r=[PK(b_), ('enb', h)], w=[('kt', h)])

        def v_proj(uT, nb, pre=False, fk='F0'):
            for g in range(2):
                def evac(b, bk, g=g):
                    if pre and (K_PRE & 2):
                        P.op('dve', lambda e: e.tensor_copy(vt[:, b, g * 512:(g + 1) * 512], psb[bk][:, 0:512]),
                             r=[PK(bk)], w=[('vt', b)])
                    else:
                        P.op('act', lambda e: e.activation(out=vt[:, b, g * 512:(g + 1) * 512],
                                                           in_=psb[bk][:, 0:512], func=AF.Copy),
                             r=[PK(bk)], w=[('vt', b)])
                as_group([('v', g, 0), ('v', g, 1)], fm_c(uT, fk), nb, [4, 4], evac)

        def kT_block(b):
            sl = slice(b * 128, (b + 1) * 128)
            bt = bank()
            pst = psb[bt][:, :].bitcast(BF16)
            for h in range(4):
                P.op('pe', lambda e, h=h: e.transpose(pst[:, h * 128:(h + 1) * 128], kt[:, h, sl], identb[:, :]),
                     r=[('kt', h), 'identb'], w=[PK(bt)])
            P.op('act', lambda e: e.activation(out=kTs[:, b % 2, :], in_=pst[:, 0:512], func=AF.Copy),
                 r=[PK(bt)], w=[('kTs', b % 2)])

        def state_block(b, pre=False):
            for hp in range(2):
                bd = bank()
                for hh in range(2):
                    h = 2 * hp + hh
                    P.op('pe', lambda e, h=h, hh=hh, bd=bd: e.matmul(
                        psb[bd][:, hh * 256:(hh + 1) * 256], kTs[:, b % 2, h * 128:(h + 1) * 128],
                        vt[:, b, h * 256:(h + 1) * 256], start=True, stop=True),
                         r=[('kTs', b % 2), ('vt', b)], w=[PK(bd)])
                for hh in range(2):
                    h = 2 * hp + hh
                    el = ebt[:, h, b * 128 + 127:b * 128 + 128]
                    ks = h % 2
                    P.op('pool', lambda e, h=h, el=el, ks=ks: e.tensor_scalar(
                        Sdec[:, ks, :], S32[:, h, :], el, 0.0, ALU.mult, ALU.add),
                         r=[('S32', h), ('eb', h)], w=[('Sdec', ks)])
                    P.op('dve', lambda e, h=h, hh=hh, el=el, ks=ks, bd=bd: e.scalar_tensor_tensor(
                        S32[:, h, :], psb[bd][:, hh * 256:(hh + 1) * 256], el, Sdec[:, ks, :], ALU.mult, ALU.add),
                         r=[PK(bd), ('Sdec', ks), ('eb', h)], w=[('S32', h)])
                    if (pre and (K_PRE & 4)) or (not pre and (K_PRE & 128)):
                        P.op('dve', lambda e, h=h: e.tensor_copy(Sbf[:, h, :], S32[:, h, :]),
                             r=[('S32', h)], w=[('Sbf', h)])
                    else:
                        P.op('act', lambda e, h=h: e.activation(out=Sbf[:, h, :], in_=S32[:, h, :], func=AF.Copy),
                             r=[('S32', h)], w=[('Sbf', h)])

        def load_tile(src, row0, nb):
            slot = tilectr['n'] % 2
            tilectr['n'] += 1
            P.dma('sp', hb[slot][:, 0:nb, :], src[row0:row0 + nb * 128, :].rearrange("(b p) d -> p b d", p=128),
                  key=('hld', slot), w=[('h', slot, b) for b in range(nb)])
            return slot

        order = []
        pre_tiles = []
        r0 = 0
        while r0 < npre_blocks:
            nb = min(NB, npre_blocks - r0)
            pre_tiles.append((r0, nb))
            r0 += nb
        pre_names = ['alr'] + [('v', g, kh) for g in range(2) for kh in range(2)] + [('k', 0), ('k', 1)]
        full_names = (['alr'] + [('v', g, kh) for g in range(2) for kh in range(2)]
                      + [('r', g, kh) for g in range(2) for kh in range(2)]
                      + [x for i in range(4) for x in (('c2', i), ('c1', i))]
                      + [('gg', i) for i in range(4)] + [('gc', i) for i in range(4)]
                      + [('k', 0), ('k', 1), ('q', 0), ('q', 1)]
                      + [('go', i) for i in range(4)] + [('co', i) for i in range(4)]
                      + [('wo', g, kh) for g in range(2) for kh in range(2)]
                      + [x for i in range(11) for x in (('ua', i), ('ub', i))]
                      + [('dn', g, kp) for g in range(2) for kp in range(6)])
        assert nfull_blocks % NB == 0
        nfull_tiles = nfull_blocks // NB
        for _ in pre_tiles:
            order += [pid[n] for n in pre_names]
        for _ in range(nfull_tiles):
            order += [pid[n] for n in full_names]
        stream['order'] = order

        pre_slot = {}
        full_slot = {}

        def pre_load(i_):
            if i_ < len(pre_tiles):
                blk0, nb = pre_tiles[i_]
                pre_slot[i_] = load_tile(xp, blk0 * 128, nb)
            elif i_ == len(pre_tiles):
                full_slot[0] = load_tile(xf, 0, NB)

        def pre_norm(i_):
            blk0, nb = pre_tiles[i_]
            slot = pre_slot[i_]
            norm_transpose(hb[slot], slot, nb, R_G1, None, None, bool(K_PRE))

        def pre_tr(i_):
            blk0, nb = pre_tiles[i_]
            transpose_block_set(xn, 'xn', nb, R_G1, Fm[i_ % 2], 'F%d' % (i_ % 2), bool(K_PRE))

        def pre_head(i_):
            pre_norm(i_)
            pre_tr(i_)

        def pre_body(i_):
            blk0, nb = pre_tiles[i_]
            TT = nb * 128
            uT_, fk_ = Fm[i_ % 2], 'F%d' % (i_ % 2)
            pre_load(i_ + 2)
            alpha_and_k(uT_, TT, nb, False, fk_)
            if i_ + 1 < len(pre_tiles):
                pre_norm(i_ + 1)
            v_proj(uT_, nb, bool(K_PRE), fk_)
            if i_ + 1 < len(pre_tiles):
                pre_tr(i_ + 1)
            k_proj(uT_, TT, fk_)
            for b in range(nb):
                kT_block(b)
                state_block(b, True)

        pre_load(0)
        if pre_tiles:
            pre_load(1)
            pre_head(0)
        for i_ in range(len(pre_tiles)):
            pre_body(i_)
            if i_ in (1, 4, 7):
                emit_cast_group({1: 1, 4: 2, 7: 3}[i_])
            emit_diag(24)
        for gi in sorted(cast_groups):
            if gi <= 4 or len(pre_tiles) < 10:
                emit_cast_group(gi)
        emit_diag(10 ** 6)

        def tile_head_norm(ti):
            slot = full_slot[ti]
            norm_transpose(hb[slot], slot, NB, R_G1, None, None)
            return slot

        def tile_head_tr():
            transpose_block_set(xn, 'xn', NB, R_G1, Fm[0], 'F0')

        next_slot = {}
        pending_store = []

        def full_tile(ti):
            nb = NB
            TT = TTM
            if ti == 0:
                next_slot[0] = tile_head_norm(0)
                tile_head_tr()
            slot = next_slot[ti]
            H = hb[slot]
            HK = lambda b: ('h', slot, b)
            uT = Fm[0]
            if ti == 0:
                alpha_and_k(uT, TT, nb, True)
            v_proj(uT, nb)
            for g in range(2):
                def evac(b, bk, g=g):
                    P.op('act', lambda e: e.activation(out=rs[:, b, g * 512:(g + 1) * 512], in_=psb[bk][:, 0:512],
                                                       func=AF.Silu), r=[PK(bk)], w=[('rs', b)])
                as_group([('r', g, 0), ('r', g, 1)], fm_c(uT, 'F0'), nb, [4, 4], evac)
            for i in range(4):
                wp, wkey = piece(('c2', i))
                for jc in range(2):
                    b_ = bank()
                    ws_matmul(wp, wkey, jc, uT, 'F0', TT, b_)
                    P.op('act', lambda e, jc=jc, b_=b_: e.activation(out=a_B[:, jc, 0:TT], in_=psb[b_][:, 0:TT],
                                                                   func=AF.Sigmoid), r=[PK(b_)], w=[('a_B', jc)])
                wp, wkey = piece(('c1', i))
                for jc in range(2):
                    c = 2 * i + jc
                    b_ = bank()
                    ws_matmul(wp, wkey, jc, uT, 'F0', TT, b_)
                    P.op('dve', lambda e, jc=jc, c=c, b_=b_: e.tensor_tensor(
                        cbuf[:, c, HALO:HALO + TT], psb[b_][:, 0:TT], a_B[:, jc, 0:TT], ALU.mult),
                         r=[PK(b_), ('a_B', jc)], w=[('cbuf', c)])
            for nm, sg, sk in (('gg', sg1, 'sg1'), ('gc', sg2, 'sg2')):
                for i in range(4):
                    wp, wkey = piece((nm, i))
                    for jc in range(2):
                        c = 2 * i + jc
                        b_ = bank()
                        ws_matmul(wp, wkey, jc, uT, 'F0', TT, b_)
                        P.op('act', lambda e, c=c, b_=b_, sg=sg: e.activation(
                            out=sg[:, c, 0:TT], in_=psb[b_][:, 0:TT], func=AF.Sigmoid),
                             r=[PK(b_)], w=[(sk, c)])
            while pending_store:
                pending_store.pop(0)()
            if ti + 1 < nfull_tiles:
                full_slot[ti + 1] = load_tile(xf, (ti + 1) * TTM, NB)
            k_proj(uT, TT)
            for i in range(2):
                wp, wkey = piece(('q', i))
                for jc in range(2):
                    h = 2 * i + jc
                    b_ = bank()
                    ws_matmul(wp, wkey, jc, uT, 'F0', TT, b_)
                    P.op('dve', lambda e, h=h, b_=b_: e.scalar_tensor_tensor(
                        qt[:, h, 0:TT], psb[b_][:, 0:TT], 128.0 ** -0.5, ebt[:, h, 0:TT], ALU.mult, ALU.mult),
                         r=[PK(b_), ('eb', h)], w=[('qt', h)])
            def gla_scores(b):
                sl = slice(b * 128, (b + 1) * 128)
                bs = bank()
                for h in range(4):
                    P.op('pe', lambda e, h=h: e.matmul(psb[bs][:, h * 128:(h + 1) * 128], kt[:, h, sl], qt[:, h, sl],
                                                      start=True, stop=True),
                         r=[('kt', h), ('qt', h)], w=[PK(bs)])
                P.op('dve', lambda e: e.tensor_tensor(sTs[:, b % 2, :], psb[bs][:, 0:512], mask4[:, :], ALU.mult),
                     r=[PK(bs), 'mask4'], w=[('sTs', b % 2)])

            def gla_out(b):
                sl = slice(b * 128, (b + 1) * 128)
                obanks = [bank(pin=True), bank(pin=True)]
                for h in range(4):
                    bo = obanks[h // 2]
                    oc = slice((h % 2) * 256, (h % 2 + 1) * 256)
                    P.op('pe', lambda e, h=h, bo=bo, oc=oc: e.matmul(psb[bo][:, oc], qt[:, h, sl], Sbf[:, h, :],
                                                                   start=True, stop=False),
                         r=[('qt', h), ('Sbf', h)], w=[PK(bo)])
                    P.op('pe', lambda e, h=h, bo=bo, oc=oc: e.matmul(
                        psb[bo][:, oc], sTs[:, b % 2, h * 128:(h + 1) * 128], vt[:, b, h * 256:(h + 1) * 256],
                        start=False, stop=True),
                         r=[('sTs', b % 2), ('vt', b)], w=[PK(bo)])
                state_block(b)
                for h in range(4):
                    bo = obanks[h // 2]
                    oc = slice((h % 2) * 256, (h % 2 + 1) * 256)
                    P.op('act', lambda e, h=h, bo=bo, oc=oc: e.activation(out=junk[:, :], in_=psb[bo][:, oc],
                                                                        func=AF.Square, accum_out=st_ss[:, h:h + 1]),
                         r=[PK(bo)], w=['junk', 'st_ss'])
                rstd_chain(4, 256, RMS_EPS)
                for h in range(4):
                    bo = obanks[h // 2]
                    oc = slice((h % 2) * 256, (h % 2 + 1) * 256)
                    P.op('dve', lambda e, h=h, bo=bo, oc=oc: e.scalar_tensor_tensor(
                        xn[:, b, h * 256:(h + 1) * 256], psb[bo][:, oc], st_r[:, h:h + 1],
                        rs[:, b, h * 256:(h + 1) * 256], ALU.mult, ALU.mult),
                         r=[PK(bo), 'st_r', ('rs', b)], w=[('xn', b)])
                unpin(obanks[0])
                unpin(obanks[1])

            gla_scores(0)
            kT_block(0)
            for b in range(nb):
                if b + 1 < nb:
                    gla_scores(b + 1)
                    kT_block(b + 1)
                gla_out(b)
                if ti == 0 and b == 0:
                    emit_cast_group(5)
            if ti == 0:
                emit_cast_group(6)
            bm = bank(pin=True)
            bq = bank(pin=True)
            def stats_mm(c):
                P.op('pe', lambda e: e.matmul(psb[bm][:, 0:TT], onesb[:, :], ybf[:, c % 2, 0:TT],
                                              start=(c == 0), stop=(c == 7)),
                     r=['onesb', ('ybf', c % 2)], w=[PK(bm)])
                P.op('pe', lambda e: e.matmul(psb[bq][:, 0:TT], onesb[:, :], ysq[:, c % 2, 0:TT],
                                              start=(c == 0), stop=(c == 7)),
                     r=['onesb', ('ysq', c % 2)], w=[PK(bq)])

            for c in range(8):
                bc = bank()
                for j in range(NPE_TAPS):
                    P.op('pe', lambda e, c=c, j=j, bc=bc: e.matmul(
                        psb[bc][:, 0:TT], diagR[:, c * NPE_TAPS + j, :], cbuf[:, c, j:j + TT],
                        start=(j == 0), stop=(j == NPE_TAPS - 1)),
                         r=['diagR', ('cbuf', c)], w=[PK(bc)])
                for j in range(NPE_TAPS, CK):
                    P.op('dve', lambda e, c=c, j=j, bc=bc: e.scalar_tensor_tensor(
                        psb[bc][:, 0:TT], cbuf[:, c, j:j + TT], cv(c, R_CW + j), psb[bc][:, 0:TT],
                        ALU.mult, ALU.add),
                         r=[('cbuf', c), 'colv', PK(bc)], w=[PK(bc)])
                yc, yk = y32_c(c)
                P.op('act', lambda e, c=c, bc=bc, yc=yc: e.activation(out=yc[:, 0:TT], in_=psb[bc][:, 0:TT],
                                                                    func=AF.Identity, bias=cv(c, R_CB)),
                     r=[PK(bc), 'colv'], w=[yk])
                P.op('act', lambda e, c=c, bc=bc: e.activation(out=ybf[:, c % 2, 0:TT], in_=psb[bc][:, 0:TT],
                                                             func=AF.Identity, bias=cv(c, R_CB)),
                     r=[PK(bc), 'colv'], w=[('ybf', c % 2)])
                P.op('act', lambda e, c=c, bc=bc: e.activation(out=ysq[:, c % 2, 0:TT], in_=psb[bc][:, 0:TT],
                                                             func=AF.Square, bias=cv(c, R_CB)),
                     r=[PK(bc), 'colv'], w=[('ysq', c % 2)])
                if c >= 1:
                    stats_mm(c - 1)
            stats_mm(7)
            P.op('pool', lambda e: e.tensor_copy(cbuf[:, :, 0:HALO], cbuf[:, :, TT:TT + HALO]),
                 r=[('cbuf', c) for c in range(8)], w=[('cbuf', c) for c in range(8)])
            if ti == 0:
                emit_cast_group(7)
            P.op('act', lambda e: e.activation(out=a_e[:, 0, 0:TT], in_=psb[bm][:, 0:TT], func=AF.Copy),
                 r=[PK(bm)], w=[('a_e', 0)])
            P.op('dve', lambda e: e.tensor_tensor(a_e[:, 1, 0:TT], psb[bm][:, 0:TT], a_e[:, 0, 0:TT], ALU.mult),
                 r=[PK(bm), ('a_e', 0)], w=[('a_e', 1)])
            P.op('dve', lambda e: e.tensor_tensor(a_e[:, 1, 0:TT], psb[bq][:, 0:TT], a_e[:, 1, 0:TT], ALU.subtract),
                 r=[PK(bq), ('a_e', 1)], w=[('a_e', 1)])
            P.op('act', lambda e: e.activation(out=a_e[:, 2, 0:TT], in_=a_e[:, 1, 0:TT], func=AF.Ln, bias=LN_EPS),
                 r=[('a_e', 1)], w=[('a_e', 2)])
            P.op('act', lambda e: e.activation(out=psb[bm][:, 0:TT], in_=a_e[:, 2, 0:TT], func=AF.Exp, scale=-0.5),
                 r=[('a_e', 2), ('a_e', 0)], w=[PK(bm)])
            P.op('dve', lambda e: e.scalar_tensor_tensor(psb[bq][:, 0:TT], a_e[:, 0, 0:TT], -1.0, psb[bm][:, 0:TT],
                                                         ALU.mult, ALU.mult),
                 r=[('a_e', 0), PK(bm), ('a_e', 1)], w=[PK(bq)])
            P.op('act', lambda e: e.activation(out=a_e[:, 1, 0:TT], in_=a_e[:, 2, 0:TT], func=AF.Exp, scale=-0.5),
                 r=[('a_e', 2)], w=[('a_e', 1)])
            P.op('pool', lambda e: e.tensor_tensor(a_e[:, 0, 0:TT], a_e[:, 0, 0:TT], a_e[:, 1, 0:TT], ALU.mult),
                 r=[('a_e', 0), ('a_e', 1)], w=[('a_e', 0)])
            ogT = Fm[1]
            transpose_block_set(xn, 'xn', nb, R_GG, ogT, 'F1')
            sc_ = Fm[0]
            dctr = 0
            for c in (0, 1, 3, 2, 4, 6, 5, 7):
                yc, yk = y32_c(c)
                if c in (2, 5, 7):
                    P.op('pool', lambda e, yc=yc: e.tensor_tensor(a_e[:, 2, 0:TT], yc[:, 0:TT], a_e[:, 1, 0:TT],
                                                                  ALU.mult),
                         r=[yk, ('a_e', 1)], w=[('a_e', 2)])
                    P.op('pool', lambda e: e.tensor_tensor(a_e[:, 2, 0:TT], a_e[:, 2, 0:TT], a_e[:, 0, 0:TT],
                                                           ALU.subtract),
                         r=[('a_e', 2), ('a_e', 0)], w=[('a_e', 2)])
                    P.op('act', lambda e, c=c: e.activation(out=sc_[:, c, 0:TT], in_=a_e[:, 2, 0:TT],
                                                           func=AF.Silu, scale=cv(c, R_LG), bias=cv(c, R_LB)),
                         r=[('a_e', 2), 'colv'], w=[('F0', c)])
                    continue
                zs = dctr % 2
                dctr += 1
                P.op('dve', lambda e, c=c, zs=zs, yc=yc: e.tensor_tensor(mtmp[:, zs, 0:TT], yc[:, 0:TT],
                                                                        psb[bm][:, 0:TT], ALU.mult),
                     r=[yk, PK(bm)], w=[('mtmp', zs)])
                P.op('dve', lambda e, c=c, zs=zs: e.tensor_tensor(mtmp[:, zs, 0:TT], mtmp[:, zs, 0:TT],
                                                                 psb[bq][:, 0:TT], ALU.add),
                     r=[('mtmp', zs), PK(bq)], w=[('mtmp', zs)])
                P.op('act', lambda e, c=c, zs=zs: e.activation(out=sc_[:, c, 0:TT], in_=mtmp[:, zs, 0:TT],
                                                             func=AF.Silu, scale=cv(c, R_LG), bias=cv(c, R_LB)),
                     r=[('mtmp', zs), 'colv'], w=[('F0', c)])
            unpin(bm)
            unpin(bq)
            for i in range(4):
                wp, wkey = piece(('go', i))
                for jc in range(2):
                    m = 2 * i + jc
                    b_ = bank()
                    ws_matmul(wp, wkey, jc, ogT, 'F1', TT, b_)
                    P.op('dve', lambda e, m=m, b_=b_: e.tensor_tensor(sg1[:, m, 0:TT], psb[b_][:, 0:TT],
                                                                    sg1[:, m, 0:TT], ALU.mult),
                         r=[PK(b_), ('sg1', m)], w=[('sg1', m)])
            mg = Fm[1]
            for i in range(4):
                wp, wkey = piece(('co', i))
                for jc in range(2):
                    m = 2 * i + jc
                    b_ = bank()
                    ws_matmul(wp, wkey, jc, sc_, 'F0', TT, b_)
                    P.op('dve', lambda e, m=m, b_=b_: e.tensor_tensor(mtmp[:, m % 2, 0:TT], psb[b_][:, 0:TT],
                                                                    sg2[:, m, 0:TT], ALU.mult),
                         r=[PK(b_), ('sg2', m)], w=[('mtmp', m % 2)])
                    P.op('pool', lambda e, m=m: e.tensor_tensor(mg[:, m, 0:TT], mtmp[:, m % 2, 0:TT],
                                                               sg1[:, m, 0:TT], ALU.add),
                         r=[('mtmp', m % 2), ('sg1', m)], w=[('F1', m)])
            if ti == 0:
                emit_cast_group(8)
            for g in range(2):
                def evac(b, bk, g=g):
                    P.op('dve', lambda e: e.tensor_tensor(H[:, b, g * 512:(g + 1) * 512], psb[bk][:, 0:512],
                                                          H[:, b, g * 512:(g + 1) * 512], ALU.add),
                         r=[PK(bk), HK(b)], w=[HK(b)])
                as_group([('wo', g, 0), ('wo', g, 1)], fm_c(mg, 'F1'), nb, [4, 4], evac)
            u2 = Fm[0]
            norm_transpose(H, slot, nb, R_G2, u2, 'F0')
            if ti + 1 < nfull_tiles:
                next_slot[ti + 1] = tile_head_norm(ti + 1)
            P.op('pool', lambda e: e.tensor_copy(Fm0[:, :, 0:2], u2h[:, :, :]), r=['u2h'], w=['F0h'])
            P.op('pool', lambda e: e.tensor_copy(u2h[:, :, :], Fm0[:, :, TT:TT + 2]),
                 r=[('F0', c) for c in range(8)] + ['F0h'], w=['u2h'])
            for i in range(11):
                wpa, wka = piece(('ua', i))
                abanks = []
                for jc in range(2):
                    ba = bank(pin=True)
                    abanks.append(ba)
                    for k in range(8):
                        P.op('pe', lambda e, k=k, jc=jc, ba=ba, wpa=wpa: e.matmul(
                            psb[ba][:, 0:TT + 2], wpa[:, k, jc * 128:(jc + 1) * 128], Fm0[:, k, 0:TT + 2],
                            start=(k == 0), stop=(k == 7)),
                             r=[wka, ('F0', k), 'F0h'], w=[PK(ba)])
                wpb, wkb = piece(('ub', i))
                for jc in range(2):
                    m = 2 * i + jc
                    ba = abanks[jc]
                    bb = bank()
                    ws_matmul(wpb, wkb, jc, u2, 'F0', TT, bb)
                    fa = m % 3
                    A_ = a_e[:, fa, :]
                    AK = ('a_e', fa)
                    P.op('act', lambda e, m=m, ba=ba, A_=A_: e.activation(
                        out=A_[:, 0:TT], in_=psb[ba][:, 2:TT + 2], func=AF.Identity, scale=ffv(m, 2),
                        bias=ffv(m, 3)), r=[PK(ba), 'colv'], w=[AK])
                    P.op('dve', lambda e, m=m, ba=ba, A_=A_: e.scalar_tensor_tensor(
                        A_[:, 0:TT], psb[ba][:, 1:TT + 1], ffv(m, 1), A_[:, 0:TT], ALU.mult, ALU.add),
                         r=[PK(ba), 'colv', AK], w=[AK])
                    P.op('dve', lambda e, m=m, ba=ba, A_=A_: e.scalar_tensor_tensor(
                        A_[:, 0:TT], psb[ba][:, 0:TT], ffv(m, 0), A_[:, 0:TT], ALU.mult, ALU.add),
                         r=[PK(ba), 'colv', AK], w=[AK])
                    unpin(ba)
                    P.op('act', lambda e, m=m, A_=A_: e.activation(out=a_B[:, m % 2, 0:TT], in_=A_[:, 0:TT],
                                                                 func=AF.Silu), r=[AK], w=[('a_B', m % 2)])
                    ac_, ak_ = actT_c(m)
                    P.op('dve', lambda e, m=m, bb=bb, ac_=ac_: e.tensor_tensor(ac_[:, 0:TT], psb[bb][:, 0:TT],
                                                                             a_B[:, m % 2, 0:TT], ALU.mult),
                         r=[PK(bb), ('a_B', m % 2)], w=[ak_])
            if ti + 1 < nfull_tiles:
                tile_head_tr()
            for g in range(2):
                def evac(b, bk, g=g):
                    P.op('dve', lambda e: e.tensor_tensor(H[:, b, g * 512:(g + 1) * 512], psb[bk][:, 0:512],
                                                          H[:, b, g * 512:(g + 1) * 512], ALU.add),
                         r=[PK(bk), HK(b)], w=[HK(b)])
                as_group([('dn', g, kp) for kp in range(6)], lambda kc: actT_c(kc), nb, [4, 4, 4, 4, 4, 2], evac)
            if ti + 1 < nfull_tiles:
                alpha_and_k(Fm[0], TT, nb, True)
            for b in range(nb):
                P.op('act', lambda e, b=b: e.activation(out=rs[:, b, :], in_=H[:, b, :], func=AF.Square,
                                                        accum_out=st_ss[:, b:b + 1]),
                     r=[HK(b)], w=[('rs', b), 'st_ss'])
            rstd_chain(nb, D, RMS_EPS)
            for b in range(nb):
                P.op('dve', lambda e, b=b: e.scalar_tensor_tensor(H[:, b, :], H[:, b, :], st_r[:, b:b + 1],
                                                                  fgb[:, :], ALU.mult, ALU.mult),
                     r=[HK(b), 'st_r', 'fgb'], w=[HK(b)])
            def do_store(ti=ti, slot=slot, H=H, nb=nb, TT=TT):
                P.dma('sp', y[ti * TT:(ti + 1) * TT, :].rearrange("(b p) d -> p b d", p=128), H[:, 0:nb, :],
                      key=('hst', slot), r=[('h', slot, b) for b in range(nb)], is_out=True)
            pending_store.append(do_store)
            if ti + 1 == nfull_tiles:
                while pending_store:
                    pending_store.pop(0)()

        for ti in range(nfull_tiles):
            full_tile(ti)
        assert stream['pos'] == len(order), (stream['pos'], len(order))
        P.emit()
    return nc


NPRE_BLOCKS = 32
NFULL_BLOCKS = 33
_CACHE = {}


def make_core_inputs(inputs, npre, nfull, cores=None):
    x = np.asarray(inputs["x"], dtype=np.float32)
    B, S, _ = x.shape
    meta = np.asarray(inputs["meta_tokens"], dtype=np.float32)
    sq = lambda k: np.ascontiguousarray(np.asarray(inputs[k], dtype=np.float32)[0])
    common = {
        "w_in": sq("w_in"), "w_alpha_up": sq("w_alpha_up"), "w_gla_o": sq("w_gla_o"),
        "w_conf_o": sq("w_conf_o"), "w_out": sq("w_out"), "w_up": sq("w_up"), "w_down": sq("w_down"),
        "conf_dw_w": sq("conf_dw_w"), "ffn_dw_w": sq("ffn_dw_w"),
        "identf": np.eye(128, dtype=np.float32),
        "cmask": np.triu(np.ones((128, 128), dtype=np.float32)),
    }
    for nm in ("norm_mix_g", "gla_norm_g", "conf_dw_b", "conf_ln_g", "conf_ln_b", "norm_ffn_g",
               "b_alpha", "ffn_dw_b"):
        common[nm] = sq(nm).reshape(1, -1)
    common["final_norm_g"] = np.asarray(inputs["final_norm_g"], dtype=np.float32).reshape(1, -1)
    in_maps = []
    for b in range(B):
        seq = np.concatenate([np.zeros((112, D), np.float32), meta, x[b]], axis=0)
        nblk = seq.shape[0] // 128
        ma = dict(common)
        ma["xp"] = np.zeros((max(npre, 1) * 128, D), np.float32)
        ma["xf"] = np.ascontiguousarray(seq[0:nfull * 128])
        in_maps.append(ma)
        mb = dict(common)
        mb["xp"] = np.ascontiguousarray(seq[0:max(npre, 1) * 128])
        mb["xf"] = np.ascontiguousarray(seq[(nblk - nfull) * 128:nblk * 128])
        in_maps.append(mb)
    return in_maps


def kernel(**inputs):
    x = np.asarray(inputs["x"])
    B, S, _ = x.shape
    if "nc" not in _CACHE:
        _CACHE["nc"] = build_program(NPRE_BLOCKS, NFULL_BLOCKS)
    nc = _CACHE["nc"]
    in_maps = make_core_inputs(inputs, NPRE_BLOCKS, NFULL_BLOCKS)
    res = run_bass_kernel_spmd(nc, in_maps, core_ids=list(range(2 * B)))
    out = np.empty((B, S, D), np.float32)
    half = (NFULL_BLOCKS - 1) * 128
    for b in range(B):
        ya = np.asarray(res.results[2 * b]["y"])
        yb = np.asarray(res.results[2 * b + 1]["y"])
        out[b, 0:half] = ya[128:128 + half]
        out[b, S - half:S] = yb[128:128 + half]
    return out
```
